# Optimizing a Trainium2 kernel written in Bass

```python
import math
import jax, jax.numpy as jnp
from jax import lax
import numpy as np

D_MODEL = 1024
BATCH = 8
SEQ = 4096
DEPTH = 1

EPS = 1e-6
MIX_WIDTH = D_MODEL
RET_DIM = 64
RET_WIDTH = MIX_WIDTH // 2
RET_HEADS = RET_WIDTH // RET_DIM
DIFF_DIM = 64
DIFF_WIDTH = MIX_WIDTH - RET_WIDTH
DIFF_HEADS = DIFF_WIDTH // (2 * DIFF_DIM)
IN_WIDTH = 4 * RET_WIDTH + 3 * DIFF_WIDTH
D_FF = 2816
N_BUCKETS = 32
MAX_DIST = 128
CHUNK = 128
Q_BLOCK = 128
ROPE_BASE = 10000.0
N_MOD = 9
NEG_INF = -1e30

kernel_name = "hybrid_retention_diffattn_macaron_adaln"


def rmsnorm(x, g):
    xf = x.astype(jnp.float32)
    y = xf * lax.rsqrt(jnp.mean(xf * xf, axis=-1, keepdims=True) + EPS)
    return (y * g.astype(jnp.float32)).astype(x.dtype)


def head_layernorm(x):
    xf = x.astype(jnp.float32)
    mu = jnp.mean(xf, axis=-1, keepdims=True)
    var = jnp.mean(jnp.square(xf - mu), axis=-1, keepdims=True)
    return ((xf - mu) * lax.rsqrt(var + EPS)).astype(x.dtype)


def modulate(h, shift, scale):
    return h * (1.0 + scale[:, None, :]) + shift[:, None, :]


def swiglu(h, w_in, w_down):
    g, u = jnp.split(h @ w_in, 2, axis=-1)
    return (jax.nn.silu(g) * u) @ w_down


def rope(x, pos):
    d = x.shape[-1]
    inv = ROPE_BASE ** (-jnp.arange(0, d, 2, dtype=jnp.float32) / d)
    ang = pos.astype(jnp.float32)[:, None] * inv[None, :]
    cos = jnp.cos(ang).astype(x.dtype)
    sin = jnp.sin(ang).astype(x.dtype)
    x1, x2 = x[..., : d // 2], x[..., d // 2:]
    return jnp.concatenate([x1 * cos - x2 * sin, x1 * sin + x2 * cos], axis=-1)


def t5_bucket(rel):
    n = jnp.maximum(rel, 0)
    max_exact = N_BUCKETS // 2
    nf = jnp.maximum(n, 1).astype(jnp.float32)
    large = max_exact + (jnp.log(nf / max_exact) / math.log(MAX_DIST / max_exact)
                         * (N_BUCKETS - max_exact)).astype(jnp.int32)
    large = jnp.minimum(large, N_BUCKETS - 1)
    return jnp.where(n < max_exact, n, large)


def retention(q, k, v):
    B, H, S, d = q.shape
    NC = S // CHUNK
    dt = q.dtype
    log_gamma = jnp.log1p(-(2.0 ** (-5.0 - jnp.arange(H, dtype=jnp.float32))))
    qc = q.reshape(B, H, NC, CHUNK, d)
    kc = k.reshape(B, H, NC, CHUNK, d)
    vc = v.reshape(B, H, NC, CHUNK, d)
    idx = jnp.arange(CHUNK, dtype=jnp.float32)
    dist = idx[:, None] - idx[None, :]
    decay_in = jnp.where(dist >= 0,
                         jnp.exp(log_gamma[:, None, None] * jnp.maximum(dist, 0.0)[None]),
                         0.0).astype(dt)
    s = jnp.einsum('bhncd,bhnkd->bhnck', qc, kc) * decay_in[None, :, None]
    intra = jnp.einsum('bhnck,bhnkd->bhncd', s, vc)
    zeta = jnp.exp(log_gamma[:, None] * (CHUNK - 1 - idx)[None]).astype(dt)
    kv = jnp.einsum('bhnkd,bhnke->bhnde', kc * zeta[None, :, None, :, None], vc)
    gamma_c = jnp.exp(log_gamma * CHUNK).astype(kv.dtype)[None, :, None, None]

    def step(R, kv_n):
        return R * gamma_c + kv_n, R

    R0 = jnp.zeros((B, H, d, d), dtype=kv.dtype)
    _, R_prev = lax.scan(step, R0, jnp.moveaxis(kv, 2, 0))
    R_prev = jnp.moveaxis(R_prev, 0, 2)
    xi = jnp.exp(log_gamma[:, None] * (idx + 1.0)[None]).astype(dt)
    cross = jnp.einsum('bhncd,bhnde->bhnce', qc, R_prev) * xi[None, :, None, :, None]
    return (intra + cross).reshape(B, H, S, d)


def diff_attention(q, k, v, lam, rel_bias):
    B, H, _, S, d = q.shape
    NB = S // Q_BLOCK
    scale = d ** -0.5
    kpos = jnp.arange(S)
    qb = q.reshape(B, H, 2, NB, Q_BLOCK, d)

    def block(i):
        qi = lax.dynamic_index_in_dim(qb, i, axis=3, keepdims=False)
        qpos = i * Q_BLOCK + jnp.arange(Q_BLOCK)
        rel = qpos[:, None] - kpos[None, :]
        bias = jnp.transpose(rel_bias[t5_bucket(rel)], (2, 0, 1)).astype(jnp.float32)
        logits = jnp.einsum('bhmqd,bhmkd->bhmqk', qi, k).astype(jnp.float32) * scale
        logits = logits + bias[None, :, None]
        logits = jnp.where(rel[None, None, None] >= 0, logits, NEG_INF)
        p = jax.nn.softmax(logits, axis=-1)
        a = p[:, :, 0] - lam * p[:, :, 1]
        return jnp.einsum('bhqk,bhke->bhqe', a.astype(v.dtype), v)

    out = lax.map(block, jnp.arange(NB))
    return jnp.moveaxis(out, 0, 2).reshape(B, H, S, 2 * d)


def token_mixer(h, w_in_l, lam, lambda_init, subln_g, grp_scale, w_out_l, rel_bias):
    B, S, _ = h.shape
    proj = h @ w_in_l
    splits = np.cumsum([RET_WIDTH] * 4 + [DIFF_WIDTH] * 2).tolist()
    rq, rk, rv, rg, dq, dk, dv = jnp.split(proj, splits, axis=-1)
    pos = jnp.arange(S)

    to_heads = lambda t: t.reshape(B, S, RET_HEADS, RET_DIM).transpose(0, 2, 1, 3)
    rq = rope(to_heads(rq), pos)
    rk = rope(to_heads(rk), pos) * (RET_DIM ** -0.5)
    y_ret = head_layernorm(retention(rq, rk, to_heads(rv)))
    y_ret = y_ret.transpose(0, 2, 1, 3).reshape(B, S, RET_WIDTH) * jax.nn.silu(rg)

    dq = dq.reshape(B, S, DIFF_HEADS, 2, DIFF_DIM).transpose(0, 2, 3, 1, 4)
    dk = dk.reshape(B, S, DIFF_HEADS, 2, DIFF_DIM).transpose(0, 2, 3, 1, 4)
    dv = dv.reshape(B, S, DIFF_HEADS, 2 * DIFF_DIM).transpose(0, 2, 1, 3)
    y_diff = rmsnorm(diff_attention(dq, dk, dv, lam, rel_bias), subln_g) * (1.0 - lambda_init)
    y_diff = y_diff.transpose(0, 2, 1, 3).reshape(B, S, DIFF_WIDTH)

    y = jnp.concatenate([y_ret, y_diff], axis=-1) * grp_scale
    return y @ w_out_l


def setup_inputs(seed: int = 0) -> dict:
    key = jax.random.key(seed)
    ks = jax.random.split(key, 24)
    nrm = lambda k, shape, s: jax.random.normal(k, shape, jnp.float32) * s
    gain = lambda k, shape: 1.0 + 0.05 * jax.random.normal(k, shape, jnp.float32)
    L, D = DEPTH, D_MODEL
    return {
        "x": nrm(ks[0], (BATCH, SEQ, D), 1.0),
        "c": nrm(ks[1], (BATCH, D), 1.0),
        "w_ada": nrm(ks[2], (L, D, N_MOD * D), 0.5 * D ** -0.5),
        "b_ada": nrm(ks[3], (L, N_MOD * D), 0.01),
        "norm_ffn1": gain(ks[4], (L, D)),
        "w_ffn1_in": nrm(ks[5], (L, D, 2 * D_FF), D ** -0.5),
        "w_ffn1_out": nrm(ks[6], (L, D_FF, D), D_FF ** -0.5),
        "norm_mix": gain(ks[7], (L, D)),
        "w_in": nrm(ks[8], (L, D, IN_WIDTH), D ** -0.5),
        "lambda_q1": nrm(ks[9], (L, DIFF_DIM), 0.1),
        "lambda_k1": nrm(ks[10], (L, DIFF_DIM), 0.1),
        "lambda_q2": nrm(ks[11], (L, DIFF_DIM), 0.1),
        "lambda_k2": nrm(ks[12], (L, DIFF_DIM), 0.1),
        "subln_gain": gain(ks[13], (L, 2 * DIFF_DIM)),
        "group_scale": gain(ks[14], (L, MIX_WIDTH)),
        "w_out": nrm(ks[15], (L, MIX_WIDTH, D), MIX_WIDTH ** -0.5),
        "norm_ffn2": gain(ks[16], (L, D)),
        "w_ffn2_in": nrm(ks[17], (L, D, 2 * D_FF), D ** -0.5),
        "w_ffn2_out": nrm(ks[18], (L, D_FF, D), D_FF ** -0.5),
        "rel_bias": nrm(ks[19], (N_BUCKETS, DIFF_HEADS), 0.5),
        "norm_final": gain(ks[20], (D,)),
    }


def reference(x, c, w_ada, b_ada, norm_ffn1, w_ffn1_in, w_ffn1_out, norm_mix, w_in,
              lambda_q1, lambda_k1, lambda_q2, lambda_k2, subln_gain, group_scale, w_out,
              norm_ffn2, w_ffn2_in, w_ffn2_out, rel_bias, norm_final):
    B = x.shape[0]
    for l in range(DEPTH):
        mod = (jax.nn.silu(c) @ w_ada[l] + b_ada[l]).reshape(B, N_MOD, D_MODEL)
        sh1, sc1, gt1 = mod[:, 0], mod[:, 1], mod[:, 2]
        shm, scm, gtm = mod[:, 3], mod[:, 4], mod[:, 5]
        sh2, sc2, gt2 = mod[:, 6], mod[:, 7], mod[:, 8]

        h = modulate(rmsnorm(x, norm_ffn1[l]), sh1, sc1)
        x = x + 0.5 * gt1[:, None, :] * swiglu(h, w_ffn1_in[l], w_ffn1_out[l])

        lambda_init = 0.8 - 0.6 * math.exp(-0.3 * l)
        lq1 = lambda_q1[l].astype(jnp.float32); lk1 = lambda_k1[l].astype(jnp.float32)
        lq2 = lambda_q2[l].astype(jnp.float32); lk2 = lambda_k2[l].astype(jnp.float32)
        lam = jnp.exp(jnp.sum(lq1 * lk1)) - jnp.exp(jnp.sum(lq2 * lk2)) + lambda_init
        h = modulate(rmsnorm(x, norm_mix[l]), shm, scm)
        x = x + gtm[:, None, :] * token_mixer(h, w_in[l], lam, lambda_init, subln_gain[l],
                                              group_scale[l], w_out[l], rel_bias)

        h = modulate(rmsnorm(x, norm_ffn2[l]), sh2, sc2)
        x = x + 0.5 * gt2[:, None, :] * swiglu(h, w_ffn2_in[l], w_ffn2_out[l])
    return rmsnorm(x, norm_final)
```

```python
import contextlib
import math
import numpy as np
import concourse.bass as bass
import concourse.mybir as mybir
from concourse.bass_utils import run_bass_kernel_spmd

F32 = mybir.dt.float32
BF16 = mybir.dt.bfloat16
AF = mybir.ActivationFunctionType
ALU = mybir.AluOpType
AX = mybir.AxisListType

ENGS = ("pe", "act", "dve", "pool", "sp")
D = 1024
DFF = 2816
NFF = 22
EPS = 1e-6
TT = 512
LAMBDA_INIT = 0.2
NSLOT = 4
SLOT_ELEMS = 2048


class Op:
    __slots__ = ("eng", "fn", "deps", "inc", "val", "dma", "slot")

    def __init__(self, eng, fn, dma):
        self.eng = eng
        self.fn = fn
        self.deps = []
        self.inc = False
        self.val = None
        self.dma = dma
        self.slot = None


class Sched:
    def __init__(self, n_dma_sems):
        self.ops = {e: [] for e in ENGS}
        self.bufs = {}
        self.n_dma_sems = n_dma_sems
        self.dma_count = {e: 0 for e in ENGS}
        self.dma_ops = {e: [] for e in ENGS}

    def add(self, eng, fn, reads=(), writes=(), dma=False):
        op = Op(eng, fn, dma)
        psr = [k for k in reads if isinstance(k, tuple) and k[0] == "ps" and k not in writes]
        if psr:
            writes = list(writes) + psr
        deps = []
        raw = []
        for k in reads:
            st = self.bufs.get(k)
            if st is not None and st[0] is not None:
                deps.append(st[0])
                raw.append(st[0])
        for k in writes:
            st = self.bufs.get(k)
            if st is None:
                continue
            if st[0] is not None:
                deps.append(st[0])
            deps.extend(st[1].values())
            deps.extend(st[2])
        if dma:
            nsem = self.n_dma_sems[eng]
            n = self.dma_count[eng]
            op.slot = n % nsem
            op.val = 16 * (n // nsem + 1)
            if n >= nsem:
                deps.append(self.dma_ops[eng][n - nsem])
            self.dma_count[eng] = n + 1
            self.dma_ops[eng].append(op)
        seen = set()
        for d in deps:
            if id(d) in seen:
                continue
            seen.add(id(d))
            if not d.dma and d.eng == eng:
                if eng == "pe":
                    continue
            op.deps.append(d)
            if not d.dma:
                d.inc = True
        for k in reads:
            st = self.bufs.setdefault(k, [None, {}, []])
            if dma:
                st[2].append(op)
            else:
                st[1][eng] = op
        for k in writes:
            self.bufs[k] = [op, {}, []]
        self.ops[eng].append(op)
        return op

    def emit(self, nc, final_waits=()):
        with contextlib.ExitStack() as es:
            esem = {e: es.enter_context(nc.semaphore("s_" + e)) for e in ENGS}
            dsem = {
                e: [es.enter_context(nc.semaphore("d_%s_%d" % (e, i))) for i in range(self.n_dma_sems[e])]
                for e in ENGS
                if self.dma_count[e] > 0
            }
            for e in ENGS:
                c = 0
                for op in self.ops[e]:
                    if op.dma:
                        continue
                    if op.inc:
                        c += 1
                        op.val = c
            block = es.enter_context(nc.Block())

            def token(d):
                if d.dma:
                    return (dsem[d.eng][d.slot], ("d", d.eng, d.slot), d.val)
                return (esem[d.eng], ("e", d.eng), d.val)

            def run(e, engine, extra=()):
                seen = {}
                for op in self.ops[e]:
                    for d in op.deps:
                        sem, key, val = token(d)
                        if seen.get(key, 0) >= val:
                            continue
                        seen[key] = val
                        engine.wait_ge(sem, val)
                    inst = op.fn(engine)
                    if op.dma:
                        inst.then_inc(dsem[e][op.slot], 16)
                    elif op.inc:
                        inst.then_inc(esem[e], 1)
                for d in extra:
                    sem, key, val = token(d)
                    if seen.get(key, 0) >= val:
                        continue
                    seen[key] = val
                    engine.wait_ge(sem, val)

            @block.tensor
            def _(eng):
                run("pe", eng)

            @block.scalar
            def _(eng):
                run("act", eng)

            @block.vector
            def _(eng):
                run("dve", eng)

            @block.gpsimd
            def _(eng):
                run("pool", eng, extra=list(final_waits))

            @block.sync
            def _(eng):
                run("sp", eng, extra=list(final_waits))


def _t5_bucket(n):
    n = np.maximum(n, 0)
    max_exact = 16
    nf = np.maximum(n, 1).astype(np.float32)
    large = max_exact + (np.log(nf / max_exact) / math.log(128 / max_exact) * (32 - max_exact)).astype(np.int32)
    large = np.minimum(large, 31)
    return np.where(n < max_exact, n, large)


class Pack:
    def __init__(self):
        self.items = []
        self.off = {}
        self.n = 0

    def add(self, name, arr):
        arr = np.ascontiguousarray(arr, dtype=np.float32).reshape(128, -1)
        self.off[name] = (self.n, arr.shape[1])
        self.items.append(arr)
        self.n += arr.shape[1]

    def build(self):
        return np.ascontiguousarray(np.concatenate(self.items, axis=1))


def _fm(v):
    return np.ascontiguousarray(np.asarray(v, np.float32).reshape(-1, 128).T)


def _const_layout():
    widths = [("cT", 8), ("badaT", 72), ("g1", 8), ("gm", 8), ("g2", 8), ("gf", 8), ("sublnT", 1),
              ("gsT", 8), ("b31", 4), ("ident", 128), ("DT", 1024), ("xiT", 512),
              ("zeta", 512), ("gc", 4), ("eps", 1)]
    off = {}
    n = 0
    for k, w in widths:
        off[k] = (n, w)
        n += w
    return off, n


def _const2_layout():
    widths = [("lam", 256), ("btab", 1024), ("perm", 128)]
    off = {}
    n = 0
    for k, w in widths:
        off[k] = (n, w)
        n += w
    return off, n


def _static_tables(S_len):
    idx = np.arange(128, dtype=np.float64)
    H = 8
    log_gamma = np.log1p(-(2.0 ** (-5.0 - np.arange(H, dtype=np.float64))))
    dist = idx[None, :] - idx[:, None]
    DT = np.where(dist[:, None, :] >= 0, np.exp(log_gamma[None, :, None] * np.maximum(dist, 0)[:, None, :]), 0.0) * 0.125
    xi = np.exp(log_gamma[:, None] * (idx + 1.0)[None, :])
    xiT = np.zeros((128, 4, 128))
    for c in range(4):
        xiT[0:64, c, :] = xi[2 * c][None, :]
        xiT[64:128, c, :] = xi[2 * c + 1][None, :]
    zeta = np.exp(log_gamma[:, None] * (127 - idx)[None, :]) * 0.125
    zt = np.repeat(zeta.T[:, :, None], 64, axis=2).reshape(128, 512)
    gamma_c = np.exp(log_gamma * 128)
    gc = np.zeros((128, 4))
    for c in range(4):
        gc[0:64, c] = gamma_c[2 * c]
        gc[64:128, c] = gamma_c[2 * c + 1]
    inv = (10000.0 ** (-np.arange(0, 64, 2, dtype=np.float32) / 64)).astype(np.float32)
    pos = np.arange(S_len, dtype=np.float32)
    ang = pos[:, None] * inv[None, :]
    cos = np.cos(ang.astype(np.float64)).T
    sin = np.sin(ang.astype(np.float64)).T
    cosT = np.concatenate([cos, cos, cos, cos], axis=0)
    sinT = np.concatenate([sin, sin, sin, sin], axis=0)
    cs = np.stack([cosT, sinT], axis=1).astype(np.float32)
    perm = np.zeros((128, 128), np.float32)
    for hb in (0, 64):
        for i in range(32):
            perm[hb + 32 + i, hb + i] = -1.0
            perm[hb + i, hb + 32 + i] = 1.0
    return dict(DT=DT.astype(np.float32), xiT=xiT.astype(np.float32), zeta=zt.astype(np.float32),
                gc=gc.astype(np.float32), cs=np.ascontiguousarray(cs), perm=perm)


def _bias_index():
    k = np.arange(128)[:, None]
    q = np.arange(128)[None, :]
    diag = _t5_bucket(q - k)
    near = _t5_bucket(128 + q - k)
    return diag, near, (q >= k)


def make_core_inputs(inp, b, S_len):
    st = _static_tables(S_len)
    pk = Pack()
    pk.add("cT", _fm(inp["c"][b]))
    pk.add("badaT", np.asarray(inp["b_ada"][0], np.float32).reshape(72, 128).T)
    pk.add("g1", _fm(inp["norm_ffn1"][0]))
    pk.add("gm", _fm(inp["norm_mix"][0]))
    pk.add("g2", _fm(inp["norm_ffn2"][0]))
    pk.add("gf", _fm(inp["norm_final"]))
    lam = np.concatenate([inp["lambda_q1"][0], inp["lambda_k1"][0], inp["lambda_q2"][0], inp["lambda_k2"][0]])
    pk2 = Pack()
    pk2.add("lam", np.broadcast_to(lam[None, :], (128, 256)))
    pk.add("sublnT", np.asarray(inp["subln_gain"][0], np.float32).reshape(128, 1))
    pk.add("gsT", _fm(inp["group_scale"][0]))
    rb = np.asarray(inp["rel_bias"], np.float32)
    diag, near, valid = _bias_index()
    bt = np.zeros((128, 4, 256), np.float32)
    for h in range(4):
        dg = rb[diag, h]
        dg = np.where(valid, dg, np.float32(-1e30))
        bt[:, h, 0:128] = dg
        bt[:, h, 128:256] = rb[near, h]
    pk2.add("btab", bt)
    pk.add("b31", np.broadcast_to(rb[31][None, :], (128, 4)))
    pk.add("ident", np.eye(128, dtype=np.float32))
    pk2.add("perm", st["perm"])
    pk.add("DT", st["DT"])
    pk.add("xiT", st["xiT"])
    pk.add("zeta", st["zeta"])
    pk.add("gc", st["gc"])
    pk.add("eps", np.full((128, 1), EPS, np.float32))
    off, n = _const_layout()
    assert pk.off == off and pk.n == n, (pk.off, off)
    off2, n2 = _const2_layout()
    assert pk2.off == off2 and pk2.n == n2, (pk2.off, off2)
    return {
        "consts2": pk2.build(),
        "x": np.ascontiguousarray(inp["x"][b, :S_len].T),
        "consts": pk.build(),
        "cs": st["cs"],
        "w_ada": np.ascontiguousarray(inp["w_ada"][0]),
        "w1i": np.ascontiguousarray(inp["w_ffn1_in"][0]),
        "w1o": np.ascontiguousarray(inp["w_ffn1_out"][0]),
        "w_in": np.ascontiguousarray(inp["w_in"][0]),
        "w_out": np.ascontiguousarray(inp["w_out"][0]),
        "w2i": np.ascontiguousarray(inp["w_ffn2_in"][0]),
        "w2o": np.ascontiguousarray(inp["w_ffn2_out"][0]),
    }


def build_program(S_len, stop_after=None):
    NT = S_len // TT
    NB = S_len // 128
    COFF, NCONST = _const_layout()
    COFF2, NCONST2 = _const2_layout()
    nc = bass.Bass("TRN2", target_bir_lowering=False)

    def din(name, shape, dt=F32):
        return nc.dram_tensor(name, shape, dt, kind="ExternalInput").ap()

    x_d = din("x", [D, S_len])
    consts_d = din("consts", [128, NCONST])
    consts2_d = din("consts2", [128, NCONST2])
    cs_d = din("cs", [128, 2, S_len])
    wada_d = din("w_ada", [D, 9 * D])
    w1i_d = din("w1i", [D, 2 * DFF])
    w1o_d = din("w1o", [DFF, D])
    win_d = din("w_in", [D, 3584])
    wout_d = din("w_out", [D, D])
    w2i_d = din("w2i", [D, 2 * DFF])
    w2o_d = din("w2o", [DFF, D])
    out_d = nc.dram_tensor("out", [D, S_len], F32, kind="ExternalOutput").ap()

    def scr(name, shape):
        return nc.dram_tensor(name, shape, BF16, kind="Internal").ap()

    s_1i = scr("s_1i", [NFF, 128, 8, 2, 128])
    s_1o = scr("s_1o", [16, 128, 11, 128])
    s_in = scr("s_in", [14, 128, 8, 256])
    s_out = scr("s_out", [4, 128, 8, 256])
    s_2i = scr("s_2i", [NFF, 128, 8, 2, 128])
    s_2o = scr("s_2o", [16, 128, 11, 128])

    S = Sched({"sp": 12, "pool": 16, "pe": 1, "act": 1, "dve": 1})
    es = contextlib.ExitStack()

    def sb(name, shape, dt=F32):
        return es.enter_context(nc.sbuf_tensor(name, shape, dt))

    with es:
        cst = sb("cst", [128, NCONST])
        xT2 = sb("xT", [128, 2, 8, TT])
        xpar = [0]

        class _XT:
            def __getitem__(self, idx):
                return xT2[(idx[0], xpar[0]) + tuple(idx[1:])]
        xT = _XT()

        def XK(dc):
            return ("xT", xpar[0], dc)
        hT = sb("hT", [128, 8, TT], BF16)
        actT = sb("actT", [128, 24, TT], BF16)
        ring = sb("ring", [128, NSLOT, SLOT_ELEMS], BF16)
        kc = sb("kc", [128, 4, S_len], BF16)
        vc = sb("vc", [128, NB, 4, 128], BF16)
        mixb = sb("mixb", [128, 4096])
        cs_t = sb("cs_t", [128, 2, TT])
        rstd1 = sb("rstd1", [128, TT])
        tmpA = sb("tmpA", [128, TT])
        tmpB = sb("tmpB", [128, TT])
        sq = sb("sq", [128, 2, TT], BF16)
        rstd = sb("rstd", [128, TT])
        rg = mixb[:, 0:2048].rearrange("p (a b) -> p a b", a=4)
        rvz = mixb[:, 2048:3072].bitcast(BF16).rearrange("p (a b) -> p a b", a=4)
        E = sb("E", [128, 4, 2, TT], BF16)
        Osb = sb("Osb", [128, 2, TT])
        Esum = sb("Esum", [128, TT])
        ones_f = sb("ones_f", [128, 128])
        gsub = sb("gsub", [128, 4])
        AT = mixb[:, 3072:4096].bitcast(BF16).rearrange("p (a b c) -> p a b c", a=2, b=8)
        R = sb("R", [128, 4, 128])
        Rb = sb("Rb", [128, 4, 128], BF16)
        ident_b = sb("ident_b", [128, 128], BF16)
        perm_b = sb("perm_b", [128, 128], BF16)
        ones_b = sb("ones_b", [128, 128], BF16)
        sc_b = sb("sc_b", [128, 8], BF16)
        modT = sb("modT", [128, 72])
        der = sb("der", [128, 48])
        badj = sb("badj", [128, 4, 256])
        lamt = sb("lamt", [128, 8])
        sm = sb("sm", [128, 64])
        ps = es.enter_context(nc.psum_tensor("ps", [128, 8, 512], F32))

        def C(name, a=None, b=None):
            o, w = COFF[name]
            if a is None:
                return cst[:, o:o + w]
            return cst[:, o + a:o + b]

        cst2 = actT[:, 0:6, :].rearrange("p a b -> p (a b)").bitcast(F32)
        CST2K = [("act", c_) for c_ in range(6)]

        def C2(name, a=None, b=None):
            o, w = COFF2[name]
            if a is None:
                return cst2[:, o:o + w]
            return cst2[:, o + a:o + b]

        def rqT(c):
            return actT[:, 0 + c, :]

        def rqxT(c):
            return actT[:, 4 + c, :]

        def rkT(c):
            return actT[:, 8 + c, :]

        rk_tok = actT[:, 12:16, :]
        rv_tok = actT[:, 16:20, :]
        dqT = actT[:, 20:24, :]
        yT = hT

        def actk(c):
            return ("act", c)

        def dma(eng, out, in_, reads=(), writes=()):
            return S.add(eng, lambda e: e.dma_start(out=out, in_=in_), reads, writes, dma=True)

        def mm(out, lhsT, rhs, start, stop, reads, writes, skip=False):
            S.add("pe", lambda e: e.matmul(out, lhsT=lhsT, rhs=rhs, start=start, stop=stop, skip_group_check=skip), reads, writes)

        def tr(out, in_, ident, reads, writes):
            S.add("pe", lambda e: e.transpose(out=out, in_=in_, identity=ident), reads, writes)

        def act(out, in_, func, reads, writes, scale=1.0, bias=None, accum=None):
            def f(e):
                kw = {}
                if bias is not None:
                    kw["bias"] = bias
                if accum is not None:
                    kw["accum_out"] = accum
                return e.activation(out=out, in_=in_, func=func, scale=scale, **kw)
            S.add("act", f, reads, writes)

        def tt(out, in0, in1, op, reads, writes, eng="dve"):
            S.add(eng, lambda e: e.tensor_tensor(out=out, in0=in0, in1=in1, op=op), reads, writes)

        def ts(out, in0, s1, op0, reads, writes, s2=None, op1=None, eng="dve"):
            if op1 is None:
                S.add(eng, lambda e: e.tensor_scalar(out=out, in0=in0, scalar1=s1, scalar2=None, op0=op0), reads, writes)
            else:
                S.add(eng, lambda e: e.tensor_scalar(out=out, in0=in0, scalar1=s1, scalar2=s2, op0=op0, op1=op1), reads, writes)

        def stt(out, in0, scalar, in1, op0, op1, reads, writes, eng="dve", accum=None):
            if accum is None:
                S.add(eng, lambda e: e.scalar_tensor_tensor(out=out, in0=in0, scalar=scalar, in1=in1, op0=op0, op1=op1), reads, writes)
            else:
                S.add(eng, lambda e: e.scalar_tensor_tensor(out=out, in0=in0, scalar=scalar, in1=in1, op0=op0, op1=op1, accum_out=accum), reads, writes)

        def cp(out, in_, reads, writes, eng="dve"):
            S.add(eng, lambda e: e.tensor_copy(out=out, in_=in_), reads, writes)

        def recip(out, in_, reads, writes):
            S.add("dve", lambda e: e.reciprocal(out=out, in_=in_), reads, writes)

        def red(out, in_, reads, writes):
            S.add("dve", lambda e: e.tensor_reduce(out=out, in_=in_, axis=AX.X, op=ALU.add), reads, writes)

        def PS(b):
            return ("ps", b)

        ring_ctr = [0]
        cur_tile = [0]
        srcs = {
            "ada": wada_d.rearrange("(k p) n -> p k n", p=128),
            "1i": w1i_d.rearrange("(k p) n -> p k n", p=128),
            "1o": w1o_d.rearrange("(k p) n -> p k n", p=128),
            "in": win_d.rearrange("(k p) n -> p k n", p=128),
            "out": wout_d.rearrange("(k p) n -> p k n", p=128),
            "2i": w2i_d.rearrange("(k p) n -> p k n", p=128),
            "2o": w2o_d.rearrange("(k p) n -> p k n", p=128),
        }
        scrs = {"1i": s_1i, "1o": s_1o, "in": s_in, "out": s_out, "2i": s_2i, "2o": s_2o}

        def RK(slot):
            return [("ring", slot, 0), ("ring", slot, 1)]

        def pool_eng():
            return "dve"

        def load_piece(fam, idx):
            i = ring_ctr[0]
            ring_ctr[0] += 1
            slot = i % NSLOT
            src = srcs[fam]
            if fam in ("1i", "2i"):
                nelem = 2048
                v = ring[:, slot, :].rearrange("p (k h n) -> p k h n", k=8, h=2)
                parts = [(h, v[:, :, h, :], src[:, :, h * DFF + idx * 128: h * DFF + (idx + 1) * 128]) for h in range(2)]
            elif fam in ("1o", "2o"):
                nelem = 11 * 128
                dc, h = idx // 2, idx % 2
                parts = [(None, ring[:, slot, 0:nelem].rearrange("p (k n) -> p k n", k=11),
                          src[:, h * 11:(h + 1) * 11, dc * 128:(dc + 1) * 128])]
            else:
                nelem = 2048
                parts = [(None, ring[:, slot, :].rearrange("p (k n) -> p k n", k=8), src[:, :, idx * 256:(idx + 1) * 256])]
            t_ = cur_tile[0]
            late = fam in ("2i", "2o")
            if t_ == 0 or (t_ == 1 and late):
                for (h, dst, sap) in parts:
                    dma("pool", dst, sap, writes=RK(slot) if h is None else [("ring", slot, h)])
                if fam != "ada" and not (t_ == 0 and late):
                    sd_ap = scrs[fam][idx]
                    flat = sd_ap.rearrange("p a b -> p (a b)") if len(sd_ap.shape) == 3 else sd_ap.rearrange("p a b c -> p (a b c)")
                    dma("sp", flat, ring[:, slot, 0:nelem], reads=RK(slot), writes=[("scr", fam, idx)])
            else:
                sd_ap = scrs[fam][idx]
                flat = sd_ap.rearrange("p a b -> p (a b)") if len(sd_ap.shape) == 3 else sd_ap.rearrange("p a b c -> p (a b c)")
                dma("sp", ring[:, slot, 0:nelem], flat, reads=[("scr", fam, idx)], writes=RK(slot))
            return slot

        dma("sp", cst[:, :], consts_d[:, :], writes=["cst"])
        dma("sp", cst2[:, 0:NCONST2], consts2_d[:, :], writes=CST2K)

        cp(ident_b[:, :], C("ident"), ["cst"], ["ident_b"])
        cp(perm_b[:, :], C2("perm"), CST2K, ["perm_b"])
        S.add("dve", lambda e: e.memset(ones_b[:, :], 1.0), [], ["ones_b"])
        S.add("dve", lambda e: e.memset(ones_f[:, :], 1.0), [], ["ones_f"])
        S.add("dve", lambda e: e.memset(R[:, :, :], 0.0), [], ["R"])
        S.add("dve", lambda e: e.memset(Rb[:, :, :], 0.0), [], ["Rb"])
        act(sc_b[:, :], C("cT"), AF.Silu, ["cst"], ["sc_b"])
        for h in range(4):
            o31 = COFF["b31"][0]
            ts(badj[:, h, :], C2("btab", h * 256, (h + 1) * 256), cst[:, o31 + h:o31 + h + 1], ALU.subtract, ["cst"] + CST2K, [("badj", h)])
        for h in range(4):
            og_ = COFF["gsT"][0]
            stt(gsub[:, h:h + 1], C("sublnT"), 1.0 - LAMBDA_INIT, cst[:, og_ + 4 + h:og_ + 5 + h], ALU.mult, ALU.mult, ["cst"], ["gsub"])
        tt(tmpA[:, 0:64], C2("lam", 0, 64), C2("lam", 64, 128), ALU.mult, CST2K, ["tmpA"])
        tt(tmpA[:, 64:128], C2("lam", 128, 192), C2("lam", 192, 256), ALU.mult, CST2K, ["tmpA"])
        red(lamt[:, 0:2], tmpA[:, 0:128].rearrange("p (a b) -> p a b", a=2), ["tmpA"], ["lamt"])
        act(lamt[:, 2:4], lamt[:, 0:2], AF.Exp, ["lamt"], ["lamt2"])
        tt(lamt[:, 4:5], lamt[:, 2:3], lamt[:, 3:4], ALU.subtract, ["lamt2"], ["lamt3"])
        ts(lamt[:, 5:6], lamt[:, 4:5], -1.0, ALU.mult, ["lamt3"], ["neglam"], s2=-LAMBDA_INIT, op1=ALU.add)
        neglam = lamt[:, 5:6]

        def mod_part(j0, j1, key):
            for j in range(j0, j1):
                slot = load_piece("ada", j)
                w = ring[:, slot, :].rearrange("p (k n) -> p k n", k=8)
                for cc in range(2):
                    ch = 2 * j + cc
                    for k in range(8):
                        mm(ps[:, 6, ch:ch + 1], w[:, k, cc * 128:(cc + 1) * 128], sc_b[:, k:k + 1], k == 0, k == 7,
                           [*RK(slot), "sc_b"], [PS(6)])
            o = COFF["badaT"][0]
            tt(modT[:, 2 * j0:2 * j1], ps[:, 6, 2 * j0:2 * j1], cst[:, o + 2 * j0:o + 2 * j1], ALU.add, [PS(6), "cst"], [key])

        def MOD(n):
            return modT[:, n * 8:(n + 1) * 8]

        def mod_A():
            mod_part(0, 8, "modA")
            stt(der[:, 0:8], MOD(1), 1.0, C("g1"), ALU.add, ALU.mult, ["modA", "cst"], ["der0"])

        def mod_B1():
            mod_part(8, 12, "modB1")
            ts(der[:, 8:16], MOD(2), 0.5, ALU.mult, ["modB1"], ["der1"])

        def mod_B2a():
            mod_part(12, 20, "modB")
            stt(der[:, 16:24], MOD(4), 1.0, C("gm"), ALU.add, ALU.mult, ["modB", "cst"], ["der2"])

        def mod_piece_mm(j, slot, bank):
            w = ring[:, slot, :].rearrange("p (k n) -> p k n", k=8)
            for cc in range(2):
                ch = 2 * j + cc
                for k in range(8):
                    mm(ps[:, bank, ch:ch + 1], w[:, k, cc * 128:(cc + 1) * 128], sc_b[:, k:k + 1], k == 0, k == 7,
                       [*RK(slot), "sc_b"], [PS(bank)])
            o = COFF["badaT"][0]
            tt(modT[:, 2 * j:2 * j + 2], ps[:, bank, 2 * j:2 * j + 2], cst[:, o + 2 * j:o + 2 * j + 2], ALU.add, [PS(bank), "cst"], [("mod", j)])

        def mod_B2bc_finish():
            cp(der[:, 24:32], MOD(5), [("mod", j) for j in range(20, 24)], ["der3"])
            stt(der[:, 32:40], MOD(7), 1.0, C("g2"), ALU.add, ALU.mult, [("mod", j) for j in range(28, 32)] + ["cst"], ["der4"])

        def mod_B2d():
            mod_part(32, 36, "modB2d")
            ts(der[:, 40:48], MOD(8), 0.5, ALU.mult, ["modB2d"], ["der5"])

        mod_A()

        def norm_sq(dc):
            b = dc % 2
            act(sq[:, b, :], xT[:, dc, :], AF.Square, [XK(dc)], [("sq", b)])

        def norm_mm(dc):
            b = dc % 2
            mm(ps[:, 6, :], ones_b[:, :], sq[:, b, :], dc == 0, dc == 7, [("sq", b), "ones_b"], [PS(6)])

        def norm_stat(dc):
            norm_sq(dc)
            norm_mm(dc)

        def norm_rstd(bank=6, buf=None, key="rstd"):
            buf = rstd if buf is None else buf
            act(buf[:, :], ps[:, bank, :], AF.Ln, [PS(bank), "cst"], [key], scale=1.0 / D, bias=C("eps"))
            act(buf[:, :], buf[:, :], AF.Exp, [key], [key], scale=-0.5)

        def norm_apply(Acol, shcol, DK, buf=None, key="rstd"):
            buf = rstd if buf is None else buf
            tmps = [tmpA, tmpB]
            for dc in range(8):
                t = tmps[dc % 2]
                tk = "tmpA" if dc % 2 == 0 else "tmpB"
                tt(t[:, :], xT[:, dc, :], buf[:, :], ALU.mult, [XK(dc), key], [tk])
                act(hT[:, dc, :], t[:, :], AF.Identity, [tk] + DK, [("hT", dc)], scale=Acol[:, dc:dc + 1], bias=shcol[:, dc:dc + 1])

        def ffn_in(fam, extra_units=()):
            extra_units = list(extra_units)
            silt = [tmpA, tmpB]
            slots01 = [load_piece(fam, c) for c in range(2)]
            for k in range(8):
                for c in range(2):
                    w = ring[:, slots01[c], :].rearrange("p (k h n) -> p k h n", k=8, h=2)
                    for h in range(2):
                        mm(ps[:, c * 2 + h, :], w[:, k, h, :], hT[:, k, :], k == 0, k == 7, [*RK(slots01[c]), ("hT", k)], [PS(c * 2 + h)])
            for c in range(NFF):
                pr = (c % 2) * 2
                if c >= 2:
                    slot = load_piece(fam, c)
                    w = ring[:, slot, :].rearrange("p (k h n) -> p k h n", k=8, h=2)
                    for h in range(2):
                        for k in range(8):
                            mm(ps[:, pr + h, :], w[:, k, h, :], hT[:, k, :], k == 0, k == 7, [*RK(slot), ("hT", k)], [PS(pr + h)])
                tk = "tmpA" if c % 2 == 0 else "tmpB"
                act(silt[c % 2][:, :], ps[:, pr, :], AF.Silu, [PS(pr)], [tk])
                tt(actT[:, c, :], silt[c % 2][:, :], ps[:, pr + 1, :], ALU.mult, [tk, PS(pr + 1)], [actk(c)])
                if extra_units and c == 1:
                    while extra_units:
                        extra_units.pop(0)()
            while extra_units:
                extra_units.pop(0)()

        def ffn_out(fam, gate, DK):
            for dc in range(8):
                bank = 4 + dc % 2
                for h in range(2):
                    slot = load_piece(fam, dc * 2 + h)
                    w = ring[:, slot, 0:11 * 128].rearrange("p (k n) -> p k n", k=11)
                    for kk in range(11):
                        k = h * 11 + kk
                        mm(ps[:, bank, :], w[:, kk, :], actT[:, k, :], k == 0, k == 21, [*RK(slot), actk(k)], [PS(bank)])
                if dc > 0:
                    norm_mm(dc - 1)
                stt(xT[:, dc, :], ps[:, bank, :], gate[:, dc:dc + 1], xT[:, dc, :], ALU.mult, ALU.add,
                    [PS(bank), XK(dc)] + DK, [XK(dc)])
                norm_sq(dc)
            norm_mm(7)

        x_fm = x_d.rearrange("(c p) s -> p c s", p=128)
        out_fm = out_d.rearrange("(c p) s -> p c s", p=128)

        def issue_x_load(t):
            par = t % 2
            dma("sp", xT2[:, par, :, :], x_fm[:, :, t * TT:(t + 1) * TT], writes=[("xT", par, dc) for dc in range(8)])

        def norm1_stats(bank):
            for dc in range(8):
                bq = dc % 2
                if dc % 2 == 0:
                    act(sq[:, bq, :], xT[:, dc, :], AF.Square, [XK(dc)], [("sq", bq)])
                else:
                    tt(sq[:, bq, :], xT[:, dc, :], xT[:, dc, :], ALU.mult, [XK(dc)], [("sq", bq)])
                mm(ps[:, bank, :], ones_b[:, :], sq[:, bq, :], dc == 0, dc == 7, [("sq", bq), "ones_b"], [PS(bank)])

        def store_tile(t):
            par = t % 2
            for dc in range(8):
                o = COFF["gf"][0]
                stt(xT2[:, par, dc, :], xT2[:, par, dc, :], cst[:, o + dc:o + dc + 1], rstd[:, :], ALU.mult, ALU.mult,
                    [("xT", par, dc), "rstd", "cst"], [("xT", par, dc)])
            all_stores.append(dma("pool", out_fm[:, :, t * TT:(t + 1) * TT], xT2[:, par, :, :],
                                  reads=[("xT", par, dc) for dc in range(8)]))

        def mixer(t):
            dma("sp", cs_t[:, :, :], cs_d[:, :, t * TT:(t + 1) * TT], writes=["cs_t"])
            bank_rot = [0]

            def nb():
                b = bank_rot[0] % 4
                bank_rot[0] += 1
                return b

            rot_rot = [0]

            rope_pending = []

            def rope_flush():
                while rope_pending:
                    rope_pending.pop(0)()

            def rope_epi(sec, ch, bk):
                ri = rot_rot[0] % 2
                rot_rot[0] += 1
                rb_ = 4 + ri
                if ri == 0:
                    tA, tB, kA, kB = tmpA[:, :], tmpB[:, :], "tmpA", "tmpB"
                else:
                    tA, tB, kA, kB = Osb[:, 0, :], Osb[:, 1, :], ("Osb", 0), ("Osb", 1)
                qf, kq = (rstd1, "rstd1") if ri == 0 else (Esum, "Esum")
                act(E[:, ri, 0, :], ps[:, bk, :], AF.Copy, [PS(bk)], [("E", ri, 0)])
                act(qf[:, :], ps[:, bk, :], AF.Copy, [PS(bk)], [kq])
                mm(ps[:, rb_, :], perm_b[:, :], E[:, ri, 0, :], True, True, [("E", ri, 0), "perm_b"], [PS(rb_)])
                tt(tA, qf[:, :], cs_t[:, 0, :], ALU.mult, [kq, "cs_t"], [kA])
                tt(tB, ps[:, rb_, :], cs_t[:, 1, :], ALU.mult, [PS(rb_), "cs_t"], [kB])

                def tail():
                    if sec == 0:
                        tt(tA, tA, tB, ALU.add, [kA, kB], [kA], eng=pool_eng())
                        act(rqT(ch), tA, AF.Copy, [kA], [actk(0 + ch)])
                        o = COFF["xiT"][0]
                        xi_b = cst[:, o + ch * 128:o + (ch + 1) * 128].unsqueeze(1).broadcast_to([128, 4, 128])
                        tt(rqxT(ch).rearrange("p (a b) -> p a b", a=4), tA.rearrange("p (a b) -> p a b", a=4),
                           xi_b, ALU.mult, [kA, "cst"], [actk(4 + ch)])
                    else:
                        tt(rkT(ch), tA, tB, ALU.add, [kA, kB], [actk(8 + ch)], eng=pool_eng())

                if rope_pending:
                    rope_pending.pop(0)()
                rope_pending.append(tail)

            for sec in range(4):
                if sec == 0:
                    slots = [load_piece("in", half) for half in range(2)]
                    for k in range(8):
                        for ch in range(4):
                            w = ring[:, slots[ch // 2], :].rearrange("p (k n) -> p k n", k=8)
                            mm(ps[:, ch, :], w[:, k, (ch % 2) * 128:(ch % 2 + 1) * 128], hT[:, k, :], k == 0, k == 7,
                               [*RK(slots[ch // 2]), ("hT", k)], [PS(ch)])
                    for ch in range(4):
                        rope_epi(0, ch, ch)
                    bank_rot[0] = 4
                    continue
                if sec == 2:
                    rope_flush()
                for half in range(2):
                    pj = sec * 2 + half
                    slot = load_piece("in", pj)
                    w = ring[:, slot, :].rearrange("p (k n) -> p k n", k=8)
                    if sec in (0, 1, 4, 5):
                        for cc in range(2):
                            ch = half * 2 + cc
                            bk = nb()
                            for k in range(8):
                                mm(ps[:, bk, :], w[:, k, cc * 128:(cc + 1) * 128], hT[:, k, :], k == 0, k == 7,
                                   [*RK(slot), ("hT", k)], [PS(bk)])
                            if sec in (0, 1):
                                rope_epi(sec, ch, bk)
                            elif sec == 4:
                                act(dqT[:, ch, :], ps[:, bk, :], AF.Copy, [PS(bk)], [actk(20 + ch)], scale=0.125)
                            else:
                                act(kc[:, ch, t * TT:(t + 1) * TT], ps[:, bk, :], AF.Copy, [PS(bk)], [("kc", ch, t)])
                    else:
                        for bp in range(2):
                            bk = nb()
                            for bb in range(2):
                                b = bp * 2 + bb
                                for k in range(8):
                                    mm(ps[:, bk, bb * 256:(bb + 1) * 256], hT[:, k, b * 128:(b + 1) * 128], w[:, k, :], k == 0, k == 7,
                                       [*RK(slot), ("hT", k)], [PS(bk)])
                            src = ps[:, bk, :].rearrange("p (a b) -> p a b", a=2)
                            cols = slice(half * 256, (half + 1) * 256)
                            if sec == 2:
                                cp(rv_tok[:, bp * 2:bp * 2 + 2, cols], src, [PS(bk)], [actk(16 + bp * 2), actk(17 + bp * 2)])
                                o = COFF["zeta"][0]
                                zb = cst[:, o + half * 256:o + (half + 1) * 256].unsqueeze(1).broadcast_to([128, 2, 256])
                                tt(rvz[:, bp * 2:bp * 2 + 2, cols], src, zb, ALU.mult, [PS(bk), "cst"], [("rvz", bp * 2), ("rvz", bp * 2 + 1)])
                            elif sec == 3:
                                act(rg[:, bp * 2:bp * 2 + 2, cols], src, AF.Silu, [PS(bk)], [("rg", bp * 2), ("rg", bp * 2 + 1)])
                            else:
                                for bb in range(2):
                                    b = bp * 2 + bb
                                    cp(vc[:, t * 4 + b, half * 2:half * 2 + 2, 0:128],
                                       ps[:, bk, bb * 256:(bb + 1) * 256].rearrange("p (a b) -> p a b", a=2), [PS(bk)], [("vc", t * 4 + b, half)])
            rope_flush()

            units = []
            ubank = [0]

            def make_units():
                for sec in (4, 5, 6):
                    for half in range(2):
                        pj = sec * 2 + half
                        holder = {}

                        def get_w(pj=pj, holder=holder):
                            if "slot" not in holder:
                                holder["slot"] = load_piece("in", pj)
                            slot = holder["slot"]
                            return slot, ring[:, slot, :].rearrange("p (k n) -> p k n", k=8)

                        for sub in range(2):
                            def unit(sec=sec, half=half, sub=sub, get_w=get_w):
                                slot, w = get_w()
                                bk = 5 + ubank[0] % 2
                                ubank[0] += 1
                                if sec in (4, 5):
                                    ch = half * 2 + sub
                                    for k in range(8):
                                        mm(ps[:, bk, :], w[:, k, sub * 128:(sub + 1) * 128], hT[:, k, :], k == 0, k == 7,
                                           [*RK(slot), ("hT", k)], [PS(bk)])
                                    if sec == 4:
                                        act(dqT[:, ch, :], ps[:, bk, :], AF.Copy, [PS(bk)], [actk(20 + ch)], scale=0.125)
                                    else:
                                        act(kc[:, ch, t * TT:(t + 1) * TT], ps[:, bk, :], AF.Copy, [PS(bk)], [("kc", ch, t)])
                                else:
                                    bp = sub
                                    for bb in range(2):
                                        b = bp * 2 + bb
                                        for k in range(8):
                                            mm(ps[:, bk, bb * 256:(bb + 1) * 256], hT[:, k, b * 128:(b + 1) * 128], w[:, k, :], k == 0, k == 7,
                                               [*RK(slot), ("hT", k)], [PS(bk)])
                                    for bb in range(2):
                                        b = bp * 2 + bb
                                        act(vc[:, t * 4 + b, half * 2:half * 2 + 2, 0:128],
                                            ps[:, bk, bb * 256:(bb + 1) * 256].rearrange("p (a b) -> p a b", a=2), AF.Copy,
                                            [PS(bk)], [("vc", t * 4 + b, half)])
                            units.append(unit)

            make_units()

            def run_units(n):
                for _ in range(n):
                    if units:
                        units.pop(0)()

            ps7b = ps[:, 7, 0:256].bitcast(BF16).rearrange("p (a b) -> p a b", a=4)
            for b in range(4):
                for c in range(4):
                    tr(ps7b[:, c, :], rkT(c)[:, b * 128:(b + 1) * 128], ident_b[:, :], [actk(8 + c), "ident_b"], [PS(7)])
                act(rk_tok[:, b, :].rearrange("p (a b) -> p a b", a=4), ps7b, AF.Copy, [PS(7)], [actk(12 + b)])

            def ret_S(b):
                ai = b % 2
                for hh in range(8):
                    c = hh // 2
                    pb = (hh % 2) * 64
                    mm(ps[:, hh % 2, (hh // 2) * 128:(hh // 2 + 1) * 128], rkT(c)[pb:pb + 64, b * 128:(b + 1) * 128],
                       rqT(c)[pb:pb + 64, b * 128:(b + 1) * 128], True, True, [actk(8 + c), actk(0 + c)], [PS(hh % 2)])
                o = COFF["DT"][0]
                for g in range(2):
                    tt(AT[:, ai, :, :].rearrange("p (a two) b -> p a two b", two=2)[:, :, g, :], ps[:, g, :].rearrange("p (a b) -> p a b", a=4),
                       cst[:, o:o + 1024].rearrange("p (a two b) -> p a two b", two=2, b=128)[:, :, g, :], ALU.mult,
                       [PS(g), "cst"], [("AT", ai, g)])

            def ret_acc(b):
                ai = b % 2
                bk = 2 + b % 2
                for hh in range(8):
                    c = hh // 2
                    pb = (hh % 2) * 64
                    mm(ps[:, bk, hh * 64:(hh + 1) * 64], AT[:, ai, hh, :], rv_tok[:, b, hh * 64:(hh + 1) * 64], True, False,
                       [("AT", ai, hh % 2), actk(16 + b)], [PS(bk)])
                    mm(ps[:, bk, hh * 64:(hh + 1) * 64], rqxT(c)[pb:pb + 64, b * 128:(b + 1) * 128],
                       Rb[pb:pb + 64, c, (hh % 2) * 64:(hh % 2 + 1) * 64], False, True, [actk(4 + c), "Rb"], [PS(bk)])
                for c in range(4):
                    mm(ps[:, 4, c * 128:(c + 1) * 128], rk_tok[:, b, c * 128:(c + 1) * 128], rvz[:, b, c * 128:(c + 1) * 128], True, True,
                       [actk(12 + b), ("rvz", b)], [PS(4)])

            def ret_R(b):
                og = COFF["gc"][0]
                gcb = cst[:, og:og + 4].unsqueeze(2).broadcast_to([128, 4, 128])
                tt(R[:, :, :], R[:, :, :], gcb, ALU.mult, ["R", "cst"], ["R"], eng=pool_eng())
                tt(R[:, :, :], R[:, :, :], ps[:, 4, :].rearrange("p (a b) -> p a b", a=4), ALU.add, ["R", PS(4)], ["R"])
                act(Rb[:, :, :], R[:, :, :], AF.Copy, ["R"], ["Rb"])

            def ret_LN(b):
                ai = b % 2
                bk = 2 + b % 2
                acc3 = ps[:, bk, :].rearrange("p (a b) -> p a b", a=8)
                act(tmpA[:, :], ps[:, bk, :], AF.Square, [PS(bk)], ["tmpA"])
                red(sm[:, 0:8], acc3, [PS(bk)], ["sm0"])
                red(sm[:, 8:16], tmpA[:, :].rearrange("p (a b) -> p a b", a=8), ["tmpA"], ["sm1"])
                ts(sm[:, 16:24], sm[:, 0:8], 1.0 / 64, ALU.mult, ["sm0"], ["sm2"])
                tt(sm[:, 24:32], sm[:, 16:24], sm[:, 16:24], ALU.mult, ["sm2"], ["sm3"])
                stt(sm[:, 32:40], sm[:, 8:16], 1.0 / 64, sm[:, 24:32], ALU.mult, ALU.subtract, ["sm1", "sm3"], ["sm4"])
                act(sm[:, 40:48], sm[:, 32:40], AF.Ln, ["sm4", "cst"], ["sm5"], bias=C("eps"))
                act(sm[:, 48:56], sm[:, 40:48], AF.Exp, ["sm5"], ["sm6"], scale=-0.5)
                tt(tmpB[:, :].rearrange("p (a b) -> p a b", a=8), acc3, sm[:, 16:24].unsqueeze(2).broadcast_to([128, 8, 64]),
                   ALU.subtract, [PS(bk), "sm2"], ["tmpB"])
                tt(tmpB[:, :].rearrange("p (a b) -> p a b", a=8), tmpB[:, :].rearrange("p (a b) -> p a b", a=8),
                   sm[:, 48:56].unsqueeze(2).broadcast_to([128, 8, 64]), ALU.mult, ["tmpB", "sm6"], ["tmpB"], eng=pool_eng())
                tt(actT[:, 12 + b, :], tmpB[:, :], rg[:, b, :], ALU.mult, ["tmpB", ("rg", b)], [actk(12 + b)], eng=pool_eng())

            def ret_T(b):
                ai = b % 2
                for c in range(4):
                    tr(ps7b[:, c, :], actT[:, 12 + b, c * 128:(c + 1) * 128], ident_b[:, :], [actk(12 + b), "ident_b"], [PS(7)])
                ogs = COFF["gsT"][0]
                tt(yT[:, 0:4, b * 128:(b + 1) * 128], ps7b, cst[:, ogs:ogs + 4].unsqueeze(2).broadcast_to([128, 4, 128]), ALU.mult,
                   [PS(7), "cst"], [("hT", c_) for c_ in range(4)])

            ret_S(0)
            for b in range(4):
                ret_acc(b)
                ret_R(b)
                run_units(1)
                if b + 1 < 4:
                    ret_S(b + 1)
                ret_LN(b)
                run_units(2)
            run_units(len(units))

            o31 = COFF["b31"][0]
            nj = 4 * t + 4
            def s_stage(h, j, gi):
                qlo = max(j, 4 * t) - 4 * t
                c0 = qlo * 128
                pr = (gi % 2) * 2
                for m in range(2):
                    mm(ps[:, pr + m, c0:512], kc[m * 64:(m + 1) * 64, h, j * 128:(j + 1) * 128], dqT[m * 64:(m + 1) * 64, h, c0:512],
                       True, True, [("kc", h, j // 4), actk(20 + h)], [PS(pr + m)])
                if j >= 4 * t - 1:
                    if j == 4 * t - 1:
                        a_, b_, w_ = 0, 128, slice(128, 256)
                    else:
                        r = j - 4 * t
                        wd = 256 if r < 3 else 128
                        a_, b_, w_ = r * 128, r * 128 + wd, slice(0, wd)
                    n_ = b_ - a_
                    tt(ps[:, pr:pr + 2, a_:b_], ps[:, pr:pr + 2, a_:b_], badj[:, h, w_].unsqueeze(1).broadcast_to([128, 2, n_]), ALU.add,
                       [PS(pr), PS(pr + 1), ("badj", h)], [PS(pr), PS(pr + 1)])
                ei = gi % 4
                for m in range(2):
                    act(E[:, ei, m, c0:512], ps[:, pr + m, c0:512], AF.Exp, [PS(pr + m), "cst"], [("E", ei, m)],
                        bias=cst[:, o31 + h:o31 + h + 1])

            def pv_stage(h, j, gi):
                qlo = max(j, 4 * t) - 4 * t
                c0 = qlo * 128
                ei = gi % 4
                mm(ps[:, 4, c0:512], vc[:, j, h, :], E[:, ei, 0, c0:512], j == 0, j == nj - 1,
                   [("E", ei, 0), ("vc", j, h // 2)], [PS(4)])
                mm(ps[:, 6, c0:512], ones_b[:, :], E[:, ei, 0, c0:512], j == 0, j == nj - 1, [("E", ei, 0), "ones_b"], [PS(6)])
                mm(ps[:, 5, c0:512], vc[:, j, h, :], E[:, ei, 1, c0:512], j == 0, j == nj - 1,
                   [("E", ei, 1), ("vc", j, h // 2)], [PS(5)])
                if j == 0:
                    cp(Esum[:, :], E[:, ei, 1, :], [("E", ei, 1)], ["Esum"])
                else:
                    tt(Esum[:, c0:512], Esum[:, c0:512], E[:, ei, 1, c0:512], ALU.add, ["Esum", ("E", ei, 1)], ["Esum"])
                if j == nj - 1:
                    mm(ps[:, 7, :], ones_f[:, :], Esum[:, :], True, True, ["Esum", "ones_f"], [PS(7)])

            def epilogue(h):
                act(Osb[:, 0, :], ps[:, 4, :], AF.Copy, [PS(4)], [("Osb", 0)])
                cp(Osb[:, 1, :], ps[:, 5, :], [PS(5)], [("Osb", 1)])
                act(tmpA[:, :], ps[:, 6, :], AF.Copy, [PS(6)], ["tmpA"])
                cp(tmpB[:, :], ps[:, 7, :], [PS(7)], ["tmpB"])
                tt(rstd[:, :], tmpA[:, :], tmpB[:, :], ALU.mult, ["tmpA", "tmpB"], ["rstd"])
                tt(Osb[:, 0, :], Osb[:, 0, :], tmpB[:, :], ALU.mult, [("Osb", 0), "tmpB"], [("Osb", 0)])
                tt(Osb[:, 1, :], Osb[:, 1, :], tmpA[:, :], ALU.mult, [("Osb", 1), "tmpA"], [("Osb", 1)])
                stt(Osb[:, 0, :], Osb[:, 1, :], neglam, Osb[:, 0, :], ALU.mult, ALU.add, [("Osb", 0), ("Osb", 1), "neglam"], [("Osb", 0)])
                stt(rstd[:, :], rstd[:, :], EPS, rstd[:, :], ALU.mult, ALU.mult, ["rstd"], ["rstd"])

                def p2():
                    act(sq[:, 0, :], Osb[:, 0, :], AF.Square, [("Osb", 0)], [("sq", 0)])
                    mm(ps[:, 0, :], ones_b[:, :], sq[:, 0, :], True, True, [("sq", 0), "ones_b"], [PS(0)])
                    stt(tmpA[:, :], ps[:, 0, :], 1.0 / 128, rstd[:, :], ALU.mult, ALU.add, [PS(0), "rstd", "tmpA"], ["tmpA"])

                def p3():
                    act(tmpA[:, :], tmpA[:, :], AF.Ln, ["tmpA"], ["tmpA"])
                    act(tmpA[:, :], tmpA[:, :], AF.Exp, ["tmpA"], ["tmpA"], scale=-0.5)

                def p4():
                    stt(yT[:, 4 + h, :], Osb[:, 0, :], gsub[:, h:h + 1], tmpA[:, :], ALU.mult, ALU.mult, [("Osb", 0), "tmpA", "gsub"], [("hT", 4 + h)])

                d0 = 2 if nj < 8 else 5
                deferred.append([d0, p2])
                deferred.append([d0 + 1, p3])
                deferred.append([d0 + 2, p4])

            deferred = []

            def tick():
                for d in list(deferred):
                    d[0] -= 1
                    if d[0] <= 0:
                        d[1]()
                        deferred.remove(d)

            if t == 0:
                mslots = {}
                for idx, j_ in enumerate(range(20, 32)):
                    def ld(j_=j_):
                        mslots[j_] = load_piece("ada", j_)

                    def cmp_(j_=j_):
                        mod_piece_mm(j_, mslots[j_], 7)
                    deferred.append([idx + 1, ld])
                    deferred.append([idx + 3, cmp_])
                deferred.append([15, mod_B2bc_finish])
            gst = [(h, j) for h in range(4) for j in range(nj)]
            s_stage(gst[0][0], gst[0][1], 0)
            for gi, (h, j) in enumerate(gst):
                if gi + 1 < len(gst):
                    s_stage(gst[gi + 1][0], gst[gi + 1][1], gi + 1)
                pv_stage(h, j, gi)
                if h == 3 and j == 0:
                    for b_ in range(4):
                        deferred.append([b_ + 1 if nj >= 8 else 1, (lambda b_=b_: ret_T(b_))])
                if j == nj - 1:
                    epilogue(h)
                tick()
            while deferred:
                tick()

            for pj in range(4):
                slot = load_piece("out", pj)
                w = ring[:, slot, :].rearrange("p (k n) -> p k n", k=8)
                for cc in range(2):
                    dc = pj * 2 + cc
                    bk = dc % 4
                    for k in range(8):
                        mm(ps[:, bk, :], w[:, k, cc * 128:(cc + 1) * 128], yT[:, k, :], k == 0, k == 7, [*RK(slot), ("hT", k)], [PS(bk)])
                    if dc > 0:
                        norm_mm(dc - 1)
                    stt(xT[:, dc, :], ps[:, bk, :], der[:, 24 + dc:25 + dc], xT[:, dc, :], ALU.mult, ALU.add,
                        [PS(bk), XK(dc), "der3"], [XK(dc)])
                    norm_sq(dc)
            norm_mm(7)

        all_stores = []
        issue_x_load(0)
        xpar[0] = 0
        norm1_stats(7)
        norm_rstd(7, rstd1, "rstd1")
        norm_apply(der[:, 0:8], MOD(0), ["der0", "modA"], rstd1, "rstd1")
        pending_out = []
        for t in range(NT):
            cur_tile[0] = t
            xpar[0] = t % 2
            ffn_in("1i", extra_units=pending_out)
            pending_out = []
            if t == 0:
                mod_B1()
            ffn_out("1o", der[:, 8:16], ["der1"])
            norm_rstd()
            if t == 0:
                mod_B2a()
            norm_apply(der[:, 16:24], MOD(3), ["der2", "modB"])
            if t + 1 < NT:
                issue_x_load(t + 1)
            mixer(t)
            norm_rstd()
            norm_apply(der[:, 32:40], MOD(6), ["der4"] + [("mod", j_) for j_ in range(24, 28)])
            ffn_in("2i")
            if t == 0:
                mod_B2d()
            if t + 1 < NT:
                xpar[0] = (t + 1) % 2
                norm1_stats(7)
                norm_rstd(7, rstd1, "rstd1")
                norm_apply(der[:, 0:8], MOD(0), ["der0", "modA"], rstd1, "rstd1")
                xpar[0] = t % 2
            ffn_out("2o", der[:, 40:48], ["der5"])
            def fin(t=t):
                norm_rstd()
                store_tile(t)
            pending_out = [fin]
        for u in pending_out:
            u()

        S.emit(nc, final_waits=S.dma_ops["pool"][-16:])
    return nc


_PROG_CACHE = {}


def kernel(**inputs):
    S_len = inputs["x"].shape[1]
    B = inputs["x"].shape[0]
    if S_len not in _PROG_CACHE:
        _PROG_CACHE[S_len] = build_program(S_len)
    nc = _PROG_CACHE[S_len]
    inp = {k: np.asarray(v) for k, v in inputs.items()}
    in_maps = [make_core_inputs(inp, b, S_len) for b in range(B)]
    res = run_bass_kernel_spmd(nc, in_maps, core_ids=list(range(B)))
    out = np.stack([np.ascontiguousarray(np.asarray(r["out"], dtype=np.float32).T) for r in res.results], axis=0)
    return out
```

```python
import contextlib
import math
import numpy as np
import concourse.bass as bass
import concourse.mybir as mybir
from concourse.bass_utils import run_bass_kernel_spmd

F32 = mybir.dt.float32
BF16 = mybir.dt.bfloat16
AF = mybir.ActivationFunctionType
ALU = mybir.AluOpType
AX = mybir.AxisListType

ENGS = ("pe", "act", "dve", "pool", "sp")
D = 1024
DFF = 2816
NFF = 22
EPS = 1e-6
TT = 512
LAMBDA_INIT = 0.2
NSLOT = 4
SLOT_ELEMS = 2048


class Op:
    __slots__ = ("eng", "fn", "deps", "inc", "val", "dma", "slot")

    def __init__(self, eng, fn, dma):
        self.eng = eng
        self.fn = fn
        self.deps = []
        self.inc = False
        self.val = None
        self.dma = dma
        self.slot = None


class Sched:
    def __init__(self, n_dma_sems):
        self.ops = {e: [] for e in ENGS}
        self.bufs = {}
        self.n_dma_sems = n_dma_sems
        self.dma_count = {e: 0 for e in ENGS}
        self.dma_ops = {e: [] for e in ENGS}

    def add(self, eng, fn, reads=(), writes=(), dma=False):
        op = Op(eng, fn, dma)
        psr = [k for k in reads if isinstance(k, tuple) and k[0] == "ps" and k not in writes]
        if psr:
            writes = list(writes) + psr
        deps = []
        raw = []
        for k in reads:
            st = self.bufs.get(k)
            if st is not None and st[0] is not None:
                deps.append(st[0])
                raw.append(st[0])
        for k in writes:
            st = self.bufs.get(k)
            if st is None:
                continue
            if st[0] is not None:
                deps.append(st[0])
            deps.extend(st[1].values())
            deps.extend(st[2])
        if dma:
            nsem = self.n_dma_sems[eng]
            n = self.dma_count[eng]
            op.slot = n % nsem
            op.val = 16 * (n // nsem + 1)
            if n >= nsem:
                deps.append(self.dma_ops[eng][n - nsem])
            self.dma_count[eng] = n + 1
            self.dma_ops[eng].append(op)
        seen = set()
        for d in deps:
            if id(d) in seen:
                continue
            seen.add(id(d))
            if not d.dma and d.eng == eng:
                if eng == "pe":
                    continue
            op.deps.append(d)
            if not d.dma:
                d.inc = True
        for k in reads:
            st = self.bufs.setdefault(k, [None, {}, []])
            if dma:
                st[2].append(op)
            else:
                st[1][eng] = op
        for k in writes:
            self.bufs[k] = [op, {}, []]
        self.ops[eng].append(op)
        return op

    def emit(self, nc, final_waits=()):
        with contextlib.ExitStack() as es:
            esem = {e: es.enter_context(nc.semaphore("s_" + e)) for e in ENGS}
            dsem = {
                e: [es.enter_context(nc.semaphore("d_%s_%d" % (e, i))) for i in range(self.n_dma_sems[e])]
                for e in ENGS
                if self.dma_count[e] > 0
            }
            for e in ENGS:
                c = 0
                for op in self.ops[e]:
                    if op.dma:
                        continue
                    if op.inc:
                        c += 1
                        op.val = c
            block = es.enter_context(nc.Block())

            def token(d):
                if d.dma:
                    return (dsem[d.eng][d.slot], ("d", d.eng, d.slot), d.val)
                return (esem[d.eng], ("e", d.eng), d.val)

            def run(e, engine, extra=()):
                seen = {}
                for op in self.ops[e]:
                    for d in op.deps:
                        sem, key, val = token(d)
                        if seen.get(key, 0) >= val:
                            continue
                        seen[key] = val
                        engine.wait_ge(sem, val)
                    inst = op.fn(engine)
                    if op.dma:
                        inst.then_inc(dsem[e][op.slot], 16)
                    elif op.inc:
                        inst.then_inc(esem[e], 1)
                for d in extra:
                    sem, key, val = token(d)
                    if seen.get(key, 0) >= val:
                        continue
                    seen[key] = val
                    engine.wait_ge(sem, val)

            @block.tensor
            def _(eng):
                run("pe", eng)

            @block.scalar
            def _(eng):
                run("act", eng)

            @block.vector
            def _(eng):
                run("dve", eng)

            @block.gpsimd
            def _(eng):
                run("pool", eng, extra=list(final_waits))

            @block.sync
            def _(eng):
                run("sp", eng, extra=list(final_waits))


def _t5_bucket(n):
    n = np.maximum(n, 0)
    max_exact = 16
    nf = np.maximum(n, 1).astype(np.float32)
    large = max_exact + (np.log(nf / max_exact) / math.log(128 / max_exact) * (32 - max_exact)).astype(np.int32)
    large = np.minimum(large, 31)
    return np.where(n < max_exact, n, large)


class Pack:
    def __init__(self):
        self.items = []
        self.off = {}
        self.n = 0

    def add(self, name, arr):
        arr = np.ascontiguousarray(arr, dtype=np.float32).reshape(128, -1)
        self.off[name] = (self.n, arr.shape[1])
        self.items.append(arr)
        self.n += arr.shape[1]

    def build(self):
        return np.ascontiguousarray(np.concatenate(self.items, axis=1))


def _fm(v):
    return np.ascontiguousarray(np.asarray(v, np.float32).reshape(-1, 128).T)


def _const_layout():
    widths = [("cT", 8), ("badaT", 72), ("g1", 8), ("gm", 8), ("g2", 8), ("gf", 8), ("sublnT", 1),
              ("gsT", 8), ("b31", 4), ("ident", 128), ("DT", 1024), ("xiT", 512),
              ("zeta", 512), ("gc", 4), ("eps", 1)]
    off = {}
    n = 0
    for k, w in widths:
        off[k] = (n, w)
        n += w
    return off, n


def _const2_layout():
    widths = [("lam", 256), ("btab", 1024), ("perm", 128)]
    off = {}
    n = 0
    for k, w in widths:
        off[k] = (n, w)
        n += w
    return off, n


def _static_tables(S_len):
    idx = np.arange(128, dtype=np.float64)
    H = 8
    log_gamma = np.log1p(-(2.0 ** (-5.0 - np.arange(H, dtype=np.float64))))
    dist = idx[None, :] - idx[:, None]
    DT = np.where(dist[:, None, :] >= 0, np.exp(log_gamma[None, :, None] * np.maximum(dist, 0)[:, None, :]), 0.0) * 0.125
    xi = np.exp(log_gamma[:, None] * (idx + 1.0)[None, :])
    xiT = np.zeros((128, 4, 128))
    for c in range(4):
        xiT[0:64, c, :] = xi[2 * c][None, :]
        xiT[64:128, c, :] = xi[2 * c + 1][None, :]
    zeta = np.exp(log_gamma[:, None] * (127 - idx)[None, :]) * 0.125
    zt = np.repeat(zeta.T[:, :, None], 64, axis=2).reshape(128, 512)
    gamma_c = np.exp(log_gamma * 128)
    gc = np.zeros((128, 4))
    for c in range(4):
        gc[0:64, c] = gamma_c[2 * c]
        gc[64:128, c] = gamma_c[2 * c + 1]
    inv = (10000.0 ** (-np.arange(0, 64, 2, dtype=np.float32) / 64)).astype(np.float32)
    pos = np.arange(S_len, dtype=np.float32)
    ang = pos[:, None] * inv[None, :]
    cos = np.cos(ang.astype(np.float64)).T
    sin = np.sin(ang.astype(np.float64)).T
    cosT = np.concatenate([cos, cos, cos, cos], axis=0)
    sinT = np.concatenate([sin, sin, sin, sin], axis=0)
    cs = np.stack([cosT, sinT], axis=1).astype(np.float32)
    perm = np.zeros((128, 128), np.float32)
    for hb in (0, 64):
        for i in range(32):
            perm[hb + 32 + i, hb + i] = -1.0
            perm[hb + i, hb + 32 + i] = 1.0
    return dict(DT=DT.astype(np.float32), xiT=xiT.astype(np.float32), zeta=zt.astype(np.float32),
                gc=gc.astype(np.float32), cs=np.ascontiguousarray(cs), perm=perm)


def _bias_index():
    k = np.arange(128)[:, None]
    q = np.arange(128)[None, :]
    diag = _t5_bucket(q - k)
    near = _t5_bucket(128 + q - k)
    return diag, near, (q >= k)


def make_core_inputs(inp, b, S_len):
    st = _static_tables(S_len)
    pk = Pack()
    pk.add("cT", _fm(inp["c"][b]))
    pk.add("badaT", np.asarray(inp["b_ada"][0], np.float32).reshape(72, 128).T)
    pk.add("g1", _fm(inp["norm_ffn1"][0]))
    pk.add("gm", _fm(inp["norm_mix"][0]))
    pk.add("g2", _fm(inp["norm_ffn2"][0]))
    pk.add("gf", _fm(inp["norm_final"]))
    lam = np.concatenate([inp["lambda_q1"][0], inp["lambda_k1"][0], inp["lambda_q2"][0], inp["lambda_k2"][0]])
    pk2 = Pack()
    pk2.add("lam", np.broadcast_to(lam[None, :], (128, 256)))
    pk.add("sublnT", np.asarray(inp["subln_gain"][0], np.float32).reshape(128, 1))
    pk.add("gsT", _fm(inp["group_scale"][0]))
    rb = np.asarray(inp["rel_bias"], np.float32)
    diag, near, valid = _bias_index()
    bt = np.zeros((128, 4, 256), np.float32)
    for h in range(4):
        dg = rb[diag, h]
        dg = np.where(valid, dg, np.float32(-1e30))
        bt[:, h, 0:128] = dg
        bt[:, h, 128:256] = rb[near, h]
    pk2.add("btab", bt)
    pk.add("b31", np.broadcast_to(rb[31][None, :], (128, 4)))
    pk.add("ident", np.eye(128, dtype=np.float32))
    pk2.add("perm", st["perm"])
    pk.add("DT", st["DT"])
    pk.add("xiT", st["xiT"])
    pk.add("zeta", st["zeta"])
    pk.add("gc", st["gc"])
    pk.add("eps", np.full((128, 1), EPS, np.float32))
    off, n = _const_layout()
    assert pk.off == off and pk.n == n, (pk.off, off)
    off2, n2 = _const2_layout()
    assert pk2.off == off2 and pk2.n == n2, (pk2.off, off2)
    return {
        "consts2": pk2.build(),
        "x": np.ascontiguousarray(inp["x"][b, :S_len].T),
        "consts": pk.build(),
        "cs": st["cs"],
        "w_ada": np.ascontiguousarray(inp["w_ada"][0]),
        "w1i": np.ascontiguousarray(inp["w_ffn1_in"][0]),
        "w1o": np.ascontiguousarray(inp["w_ffn1_out"][0]),
        "w_in": np.ascontiguousarray(inp["w_in"][0]),
        "w_out": np.ascontiguousarray(inp["w_out"][0]),
        "w2i": np.ascontiguousarray(inp["w_ffn2_in"][0]),
        "w2o": np.ascontiguousarray(inp["w_ffn2_out"][0]),
    }


def build_program(S_len, stop_after=None):
    NT = S_len // TT
    NB = S_len // 128
    COFF, NCONST = _const_layout()
    COFF2, NCONST2 = _const2_layout()
    nc = bass.Bass("TRN2", target_bir_lowering=False)

    def din(name, shape, dt=F32):
        return nc.dram_tensor(name, shape, dt, kind="ExternalInput").ap()

    x_d = din("x", [D, S_len])
    consts_d = din("consts", [128, NCONST])
    consts2_d = din("consts2", [128, NCONST2])
    cs_d = din("cs", [128, 2, S_len])
    wada_d = din("w_ada", [D, 9 * D])
    w1i_d = din("w1i", [D, 2 * DFF])
    w1o_d = din("w1o", [DFF, D])
    win_d = din("w_in", [D, 3584])
    wout_d = din("w_out", [D, D])
    w2i_d = din("w2i", [D, 2 * DFF])
    w2o_d = din("w2o", [DFF, D])
    out_d = nc.dram_tensor("out", [D, S_len], F32, kind="ExternalOutput").ap()

    def scr(name, shape):
        return nc.dram_tensor(name, shape, BF16, kind="Internal").ap()

    s_1i = scr("s_1i", [NFF, 128, 8, 2, 128])
    s_1o = scr("s_1o", [16, 128, 11, 128])
    s_in = scr("s_in", [14, 128, 8, 256])
    s_out = scr("s_out", [4, 128, 8, 256])
    s_2i = scr("s_2i", [NFF, 128, 8, 2, 128])
    s_2o = scr("s_2o", [16, 128, 11, 128])

    S = Sched({"sp": 12, "pool": 16, "pe": 1, "act": 1, "dve": 1})
    es = contextlib.ExitStack()

    def sb(name, shape, dt=F32):
        return es.enter_context(nc.sbuf_tensor(name, shape, dt))

    with es:
        cst = sb("cst", [128, NCONST])
        xT2 = sb("xT", [128, 2, 8, TT])
        xpar = [0]

        class _XT:
            def __getitem__(self, idx):
                return xT2[(idx[0], xpar[0]) + tuple(idx[1:])]
        xT = _XT()

        def XK(dc):
            return ("xT", xpar[0], dc)
        hT = sb("hT", [128, 8, TT], BF16)
        actT = sb("actT", [128, 24, TT], BF16)
        ring = sb("ring", [128, NSLOT, SLOT_ELEMS], BF16)
        kc = sb("kc", [128, 4, S_len], BF16)
        vc = sb("vc", [128, NB, 4, 128], BF16)
        mixb = sb("mixb", [128, 4096])
        cs_t = sb("cs_t", [128, 2, TT])
        rstd1 = sb("rstd1", [128, TT])
        tmpA = sb("tmpA", [128, TT])
        tmpB = sb("tmpB", [128, TT])
        sq = sb("sq", [128, 2, TT], BF16)
        rstd = sb("rstd", [128, TT])
        rg = mixb[:, 0:2048].rearrange("p (a b) -> p a b", a=4)
        rvz = mixb[:, 2048:3072].bitcast(BF16).rearrange("p (a b) -> p a b", a=4)
        E = sb("E", [128, 4, 2, TT], BF16)
        Osb = sb("Osb", [128, 2, TT])
        Esum = sb("Esum", [128, TT])
        ones_f = sb("ones_f", [128, 128])
        gsub = sb("gsub", [128, 4])
        AT = mixb[:, 3072:4096].bitcast(BF16).rearrange("p (a b c) -> p a b c", a=2, b=8)
        R = sb("R", [128, 4, 128])
        Rb = sb("Rb", [128, 4, 128], BF16)
        ident_b = sb("ident_b", [128, 128], BF16)
        perm_b = sb("perm_b", [128, 128], BF16)
        ones_b = sb("ones_b", [128, 128], BF16)
        sc_b = sb("sc_b", [128, 8], BF16)
        modT = sb("modT", [128, 72])
        der = sb("der", [128, 48])
        badj = sb("badj", [128, 4, 256])
        lamt = sb("lamt", [128, 8])
        sm = sb("sm", [128, 64])
        ps = es.enter_context(nc.psum_tensor("ps", [128, 8, 512], F32))

        def C(name, a=None, b=None):
            o, w = COFF[name]
            if a is None:
                return cst[:, o:o + w]
            return cst[:, o + a:o + b]

        cst2 = actT[:, 0:6, :].rearrange("p a b -> p (a b)").bitcast(F32)
        CST2K = [("act", c_) for c_ in range(6)]

        def C2(name, a=None, b=None):
            o, w = COFF2[name]
            if a is None:
                return cst2[:, o:o + w]
            return cst2[:, o + a:o + b]

        def rqT(c):
            return actT[:, 0 + c, :]

        def rqxT(c):
            return actT[:, 4 + c, :]

        def rkT(c):
            return actT[:, 8 + c, :]

        rk_tok = actT[:, 12:16, :]
        rv_tok = actT[:, 16:20, :]
        dqT = actT[:, 20:24, :]
        yT = hT

        def actk(c):
            return ("act", c)

        def dma(eng, out, in_, reads=(), writes=()):
            return S.add(eng, lambda e: e.dma_start(out=out, in_=in_), reads, writes, dma=True)

        def mm(out, lhsT, rhs, start, stop, reads, writes, skip=False):
            S.add("pe", lambda e: e.matmul(out, lhsT=lhsT, rhs=rhs, start=start, stop=stop, skip_group_check=skip), reads, writes)

        def tr(out, in_, ident, reads, writes):
            S.add("pe", lambda e: e.transpose(out=out, in_=in_, identity=ident), reads, writes)

        def act(out, in_, func, reads, writes, scale=1.0, bias=None, accum=None):
            def f(e):
                kw = {}
                if bias is not None:
                    kw["bias"] = bias
                if accum is not None:
                    kw["accum_out"] = accum
                return e.activation(out=out, in_=in_, func=func, scale=scale, **kw)
            S.add("act", f, reads, writes)

        def tt(out, in0, in1, op, reads, writes, eng="dve"):
            S.add(eng, lambda e: e.tensor_tensor(out=out, in0=in0, in1=in1, op=op), reads, writes)

        def ts(out, in0, s1, op0, reads, writes, s2=None, op1=None, eng="dve"):
            if op1 is None:
                S.add(eng, lambda e: e.tensor_scalar(out=out, in0=in0, scalar1=s1, scalar2=None, op0=op0), reads, writes)
            else:
                S.add(eng, lambda e: e.tensor_scalar(out=out, in0=in0, scalar1=s1, scalar2=s2, op0=op0, op1=op1), reads, writes)

        def stt(out, in0, scalar, in1, op0, op1, reads, writes, eng="dve", accum=None):
            if accum is None:
                S.add(eng, lambda e: e.scalar_tensor_tensor(out=out, in0=in0, scalar=scalar, in1=in1, op0=op0, op1=op1), reads, writes)
            else:
                S.add(eng, lambda e: e.scalar_tensor_tensor(out=out, in0=in0, scalar=scalar, in1=in1, op0=op0, op1=op1, accum_out=accum), reads, writes)

        def cp(out, in_, reads, writes, eng="dve"):
            S.add(eng, lambda e: e.tensor_copy(out=out, in_=in_), reads, writes)

        def recip(out, in_, reads, writes):
            S.add("dve", lambda e: e.reciprocal(out=out, in_=in_), reads, writes)

        def red(out, in_, reads, writes):
            S.add("dve", lambda e: e.tensor_reduce(out=out, in_=in_, axis=AX.X, op=ALU.add), reads, writes)

        def PS(b):
            return ("ps", b)

        ring_ctr = [0]
        cur_tile = [0]
        srcs = {
            "ada": wada_d.rearrange("(k p) n -> p k n", p=128),
            "1i": w1i_d.rearrange("(k p) n -> p k n", p=128),
            "1o": w1o_d.rearrange("(k p) n -> p k n", p=128),
            "in": win_d.rearrange("(k p) n -> p k n", p=128),
            "out": wout_d.rearrange("(k p) n -> p k n", p=128),
            "2i": w2i_d.rearrange("(k p) n -> p k n", p=128),
            "2o": w2o_d.rearrange("(k p) n -> p k n", p=128),
        }
        scrs = {"1i": s_1i, "1o": s_1o, "in": s_in, "out": s_out, "2i": s_2i, "2o": s_2o}

        def RK(slot):
            return [("ring", slot, 0), ("ring", slot, 1)]

        def pool_eng():
            return "dve"

        def load_piece(fam, idx):
            i = ring_ctr[0]
            ring_ctr[0] += 1
            slot = i % NSLOT
            src = srcs[fam]
            if fam in ("1i", "2i"):
                nelem = 2048
                v = ring[:, slot, :].rearrange("p (k h n) -> p k h n", k=8, h=2)
                parts = [(h, v[:, :, h, :], src[:, :, h * DFF + idx * 128: h * DFF + (idx + 1) * 128]) for h in range(2)]
            elif fam in ("1o", "2o"):
                nelem = 11 * 128
                dc, h = idx // 2, idx % 2
                parts = [(None, ring[:, slot, 0:nelem].rearrange("p (k n) -> p k n", k=11),
                          src[:, h * 11:(h + 1) * 11, dc * 128:(dc + 1) * 128])]
            else:
                nelem = 2048
                parts = [(None, ring[:, slot, :].rearrange("p (k n) -> p k n", k=8), src[:, :, idx * 256:(idx + 1) * 256])]
            t_ = cur_tile[0]
            late = fam in ("2i", "2o")
            if t_ == 0 or (t_ == 1 and late):
                for (h, dst, sap) in parts:
                    dma("pool", dst, sap, writes=RK(slot) if h is None else [("ring", slot, h)])
                if fam != "ada" and not (t_ == 0 and late):
                    sd_ap = scrs[fam][idx]
                    flat = sd_ap.rearrange("p a b -> p (a b)") if len(sd_ap.shape) == 3 else sd_ap.rearrange("p a b c -> p (a b c)")
                    dma("sp", flat, ring[:, slot, 0:nelem], reads=RK(slot), writes=[("scr", fam, idx)])
            else:
                sd_ap = scrs[fam][idx]
                flat = sd_ap.rearrange("p a b -> p (a b)") if len(sd_ap.shape) == 3 else sd_ap.rearrange("p a b c -> p (a b c)")
                dma("sp", ring[:, slot, 0:nelem], flat, reads=[("scr", fam, idx)], writes=RK(slot))
            return slot

        dma("sp", cst[:, :], consts_d[:, :], writes=["cst"])
        dma("sp", cst2[:, 0:NCONST2], consts2_d[:, :], writes=CST2K)

        cp(ident_b[:, :], C("ident"), ["cst"], ["ident_b"])
        cp(perm_b[:, :], C2("perm"), CST2K, ["perm_b"])
        S.add("dve", lambda e: e.memset(ones_b[:, :], 1.0), [], ["ones_b"])
        S.add("dve", lambda e: e.memset(ones_f[:, :], 1.0), [], ["ones_f"])
        S.add("dve", lambda e: e.memset(R[:, :, :], 0.0), [], ["R"])
        S.add("dve", lambda e: e.memset(Rb[:, :, :], 0.0), [], ["Rb"])
        act(sc_b[:, :], C("cT"), AF.Silu, ["cst"], ["sc_b"])
        for h in range(4):
            o31 = COFF["b31"][0]
            ts(badj[:, h, :], C2("btab", h * 256, (h + 1) * 256), cst[:, o31 + h:o31 + h + 1], ALU.subtract, ["cst"] + CST2K, [("badj", h)])
        for h in range(4):
            og_ = COFF["gsT"][0]
            stt(gsub[:, h:h + 1], C("sublnT"), 1.0 - LAMBDA_INIT, cst[:, og_ + 4 + h:og_ + 5 + h], ALU.mult, ALU.mult, ["cst"], ["gsub"])
        tt(tmpA[:, 0:64], C2("lam", 0, 64), C2("lam", 64, 128), ALU.mult, CST2K, ["tmpA"])
        tt(tmpA[:, 64:128], C2("lam", 128, 192), C2("lam", 192, 256), ALU.mult, CST2K, ["tmpA"])
        red(lamt[:, 0:2], tmpA[:, 0:128].rearrange("p (a b) -> p a b", a=2), ["tmpA"], ["lamt"])
        act(lamt[:, 2:4], lamt[:, 0:2], AF.Exp, ["lamt"], ["lamt2"])
        tt(lamt[:, 4:5], lamt[:, 2:3], lamt[:, 3:4], ALU.subtract, ["lamt2"], ["lamt3"])
        ts(lamt[:, 5:6], lamt[:, 4:5], -1.0, ALU.mult, ["lamt3"], ["neglam"], s2=-LAMBDA_INIT, op1=ALU.add)
        neglam = lamt[:, 5:6]

        def mod_part(j0, j1, key):
            for j in range(j0, j1):
                slot = load_piece("ada", j)
                w = ring[:, slot, :].rearrange("p (k n) -> p k n", k=8)
                for cc in range(2):
                    ch = 2 * j + cc
                    for k in range(8):
                        mm(ps[:, 6, ch:ch + 1], w[:, k, cc * 128:(cc + 1) * 128], sc_b[:, k:k + 1], k == 0, k == 7,
                           [*RK(slot), "sc_b"], [PS(6)])
            o = COFF["badaT"][0]
            tt(modT[:, 2 * j0:2 * j1], ps[:, 6, 2 * j0:2 * j1], cst[:, o + 2 * j0:o + 2 * j1], ALU.add, [PS(6), "cst"], [key])

        def MOD(n):
            return modT[:, n * 8:(n + 1) * 8]

        def mod_A():
            mod_part(0, 8, "modA")
            stt(der[:, 0:8], MOD(1), 1.0, C("g1"), ALU.add, ALU.mult, ["modA", "cst"], ["der0"])

        def mod_B1():
            mod_part(8, 12, "modB1")
            ts(der[:, 8:16], MOD(2), 0.5, ALU.mult, ["modB1"], ["der1"])

        def mod_B2a():
            mod_part(12, 20, "modB")
            stt(der[:, 16:24], MOD(4), 1.0, C("gm"), ALU.add, ALU.mult, ["modB", "cst"], ["der2"])

        def mod_piece_mm(j, slot, bank):
            w = ring[:, slot, :].rearrange("p (k n) -> p k n", k=8)
            for cc in range(2):
                ch = 2 * j + cc
                for k in range(8):
                    mm(ps[:, bank, ch:ch + 1], w[:, k, cc * 128:(cc + 1) * 128], sc_b[:, k:k + 1], k == 0, k == 7,
                       [*RK(slot), "sc_b"], [PS(bank)])
            o = COFF["badaT"][0]
            tt(modT[:, 2 * j:2 * j + 2], ps[:, bank, 2 * j:2 * j + 2], cst[:, o + 2 * j:o + 2 * j + 2], ALU.add, [PS(bank), "cst"], [("mod", j)])

        def mod_B2bc_finish():
            cp(der[:, 24:32], MOD(5), [("mod", j) for j in range(20, 24)], ["der3"])
            stt(der[:, 32:40], MOD(7), 1.0, C("g2"), ALU.add, ALU.mult, [("mod", j) for j in range(28, 32)] + ["cst"], ["der4"])

        def mod_B2d():
            mod_part(32, 36, "modB2d")
            ts(der[:, 40:48], MOD(8), 0.5, ALU.mult, ["modB2d"], ["der5"])

        mod_A()

        def norm_sq(dc):
            b = dc % 2
            act(sq[:, b, :], xT[:, dc, :], AF.Square, [XK(dc)], [("sq", b)])

        def norm_mm(dc):
            b = dc % 2
            mm(ps[:, 6, :], ones_b[:, :], sq[:, b, :], dc == 0, dc == 7, [("sq", b), "ones_b"], [PS(6)])

        def norm_stat(dc):
            norm_sq(dc)
            norm_mm(dc)

        def norm_rstd(bank=6, buf=None, key="rstd"):
            buf = rstd if buf is None else buf
            act(buf[:, :], ps[:, bank, :], AF.Ln, [PS(bank), "cst"], [key], scale=1.0 / D, bias=C("eps"))
            act(buf[:, :], buf[:, :], AF.Exp, [key], [key], scale=-0.5)

        def norm_apply(Acol, shcol, DK, buf=None, key="rstd"):
            buf = rstd if buf is None else buf
            tmps = [tmpA, tmpB]
            for dc in range(8):
                t = tmps[dc % 2]
                tk = "tmpA" if dc % 2 == 0 else "tmpB"
                tt(t[:, :], xT[:, dc, :], buf[:, :], ALU.mult, [XK(dc), key], [tk])
                act(hT[:, dc, :], t[:, :], AF.Identity, [tk] + DK, [("hT", dc)], scale=Acol[:, dc:dc + 1], bias=shcol[:, dc:dc + 1])

        def ffn_in(fam, extra_units=()):
            extra_units = list(extra_units)
            silt = [tmpA, tmpB]
            slots01 = [load_piece(fam, c) for c in range(2)]
            for k in range(8):
                for c in range(2):
                    w = ring[:, slots01[c], :].rearrange("p (k h n) -> p k h n", k=8, h=2)
                    for h in range(2):
                        mm(ps[:, c * 2 + h, :], w[:, k, h, :], hT[:, k, :], k == 0, k == 7, [*RK(slots01[c]), ("hT", k)], [PS(c * 2 + h)])
            for c in range(NFF):
                pr = (c % 2) * 2
                if c >= 2:
                    slot = load_piece(fam, c)
                    w = ring[:, slot, :].rearrange("p (k h n) -> p k h n", k=8, h=2)
                    for h in range(2):
                        for k in range(8):
                            mm(ps[:, pr + h, :], w[:, k, h, :], hT[:, k, :], k == 0, k == 7, [*RK(slot), ("hT", k)], [PS(pr + h)])
                tk = "tmpA" if c % 2 == 0 else "tmpB"
                act(silt[c % 2][:, :], ps[:, pr, :], AF.Silu, [PS(pr)], [tk])
                tt(actT[:, c, :], silt[c % 2][:, :], ps[:, pr + 1, :], ALU.mult, [tk, PS(pr + 1)], [actk(c)])
                if extra_units and c == 1:
                    while extra_units:
                        extra_units.pop(0)()
            while extra_units:
                extra_units.pop(0)()

        def ffn_out(fam, gate, DK):
            for dc in range(8):
                bank = 4 + dc % 2
                for h in range(2):
                    slot = load_piece(fam, dc * 2 + h)
                    w = ring[:, slot, 0:11 * 128].rearrange("p (k n) -> p k n", k=11)
                    for kk in range(11):
                        k = h * 11 + kk
                        mm(ps[:, bank, :], w[:, kk, :], actT[:, k, :], k == 0, k == 21, [*RK(slot), actk(k)], [PS(bank)])
                if dc > 0:
                    norm_mm(dc - 1)
                stt(xT[:, dc, :], ps[:, bank, :], gate[:, dc:dc + 1], xT[:, dc, :], ALU.mult, ALU.add,
                    [PS(bank), XK(dc)] + DK, [XK(dc)])
                norm_sq(dc)
            norm_mm(7)

        x_fm = x_d.rearrange("(c p) s -> p c s", p=128)
        out_fm = out_d.rearrange("(c p) s -> p c s", p=128)

        def issue_x_load(t):
            par = t % 2
            dma("sp", xT2[:, par, :, :], x_fm[:, :, t * TT:(t + 1) * TT], writes=[("xT", par, dc) for dc in range(8)])

        def norm1_stats(bank):
            for dc in range(8):
                bq = dc % 2
                if dc % 2 == 0:
                    act(sq[:, bq, :], xT[:, dc, :], AF.Square, [XK(dc)], [("sq", bq)])
                else:
                    tt(sq[:, bq, :], xT[:, dc, :], xT[:, dc, :], ALU.mult, [XK(dc)], [("sq", bq)])
                mm(ps[:, bank, :], ones_b[:, :], sq[:, bq, :], dc == 0, dc == 7, [("sq", bq), "ones_b"], [PS(bank)])

        def store_tile(t):
            par = t % 2
            for dc in range(8):
                o = COFF["gf"][0]
                stt(xT2[:, par, dc, :], xT2[:, par, dc, :], cst[:, o + dc:o + dc + 1], rstd[:, :], ALU.mult, ALU.mult,
                    [("xT", par, dc), "rstd", "cst"], [("xT", par, dc)])
            all_stores.append(dma("pool", out_fm[:, :, t * TT:(t + 1) * TT], xT2[:, par, :, :],
                                  reads=[("xT", par, dc) for dc in range(8)]))

        def mixer(t):
            dma("sp", cs_t[:, :, :], cs_d[:, :, t * TT:(t + 1) * TT], writes=["cs_t"])
            bank_rot = [0]

            def nb():
                b = bank_rot[0] % 4
                bank_rot[0] += 1
                return b

            rot_rot = [0]

            rope_pending = []

            def rope_flush():
                while rope_pending:
                    rope_pending.pop(0)()

            def rope_epi(sec, ch, bk):
                ri = rot_rot[0] % 2
                rot_rot[0] += 1
                rb_ = 4 + ri
                if ri == 0:
                    tA, tB, kA, kB = tmpA[:, :], tmpB[:, :], "tmpA", "tmpB"
                else:
                    tA, tB, kA, kB = Osb[:, 0, :], Osb[:, 1, :], ("Osb", 0), ("Osb", 1)
                qf, kq = (rstd1, "rstd1") if ri == 0 else (Esum, "Esum")
                act(E[:, ri, 0, :], ps[:, bk, :], AF.Copy, [PS(bk)], [("E", ri, 0)])
                act(qf[:, :], ps[:, bk, :], AF.Copy, [PS(bk)], [kq])
                mm(ps[:, rb_, :], perm_b[:, :], E[:, ri, 0, :], True, True, [("E", ri, 0), "perm_b"], [PS(rb_)])
                tt(tA, qf[:, :], cs_t[:, 0, :], ALU.mult, [kq, "cs_t"], [kA])
                tt(tB, ps[:, rb_, :], cs_t[:, 1, :], ALU.mult, [PS(rb_), "cs_t"], [kB])

                def tail():
                    if sec == 0:
                        tt(tA, tA, tB, ALU.add, [kA, kB], [kA], eng=pool_eng())
                        act(rqT(ch), tA, AF.Copy, [kA], [actk(0 + ch)])
                        o = COFF["xiT"][0]
                        xi_b = cst[:, o + ch * 128:o + (ch + 1) * 128].unsqueeze(1).broadcast_to([128, 4, 128])
                        tt(rqxT(ch).rearrange("p (a b) -> p a b", a=4), tA.rearrange("p (a b) -> p a b", a=4),
                           xi_b, ALU.mult, [kA, "cst"], [actk(4 + ch)])
                    else:
                        tt(rkT(ch), tA, tB, ALU.add, [kA, kB], [actk(8 + ch)], eng=pool_eng())

                if rope_pending:
                    rope_pending.pop(0)()
                rope_pending.append(tail)

            for sec in range(4):
                if sec == 0:
                    slots = [load_piece("in", half) for half in range(2)]
                    for k in range(8):
                        for ch in range(4):
                            w = ring[:, slots[ch // 2], :].rearrange("p (k n) -> p k n", k=8)
                            mm(ps[:, ch, :], w[:, k, (ch % 2) * 128:(ch % 2 + 1) * 128], hT[:, k, :], k == 0, k == 7,
                               [*RK(slots[ch // 2]), ("hT", k)], [PS(ch)])
                    for ch in range(4):
                        rope_epi(0, ch, ch)
                    bank_rot[0] = 4
                    continue
                if sec == 2:
                    rope_flush()
                for half in range(2):
                    pj = sec * 2 + half
                    slot = load_piece("in", pj)
                    w = ring[:, slot, :].rearrange("p (k n) -> p k n", k=8)
                    if sec in (0, 1, 4, 5):
                        for cc in range(2):
                            ch = half * 2 + cc
                            bk = nb()
                            for k in range(8):
                                mm(ps[:, bk, :], w[:, k, cc * 128:(cc + 1) * 128], hT[:, k, :], k == 0, k == 7,
                                   [*RK(slot), ("hT", k)], [PS(bk)])
                            if sec in (0, 1):
                                rope_epi(sec, ch, bk)
                            elif sec == 4:
                                act(dqT[:, ch, :], ps[:, bk, :], AF.Copy, [PS(bk)], [actk(20 + ch)], scale=0.125)
                            else:
                                act(kc[:, ch, t * TT:(t + 1) * TT], ps[:, bk, :], AF.Copy, [PS(bk)], [("kc", ch, t)])
                    else:
                        for bp in range(2):
                            bk = nb()
                            for bb in range(2):
                                b = bp * 2 + bb
                                for k in range(8):
                                    mm(ps[:, bk, bb * 256:(bb + 1) * 256], hT[:, k, b * 128:(b + 1) * 128], w[:, k, :], k == 0, k == 7,
                                       [*RK(slot), ("hT", k)], [PS(bk)])
                            src = ps[:, bk, :].rearrange("p (a b) -> p a b", a=2)
                            cols = slice(half * 256, (half + 1) * 256)
                            if sec == 2:
                                cp(rv_tok[:, bp * 2:bp * 2 + 2, cols], src, [PS(bk)], [actk(16 + bp * 2), actk(17 + bp * 2)])
                                o = COFF["zeta"][0]
                                zb = cst[:, o + half * 256:o + (half + 1) * 256].unsqueeze(1).broadcast_to([128, 2, 256])
                                tt(rvz[:, bp * 2:bp * 2 + 2, cols], src, zb, ALU.mult, [PS(bk), "cst"], [("rvz", bp * 2), ("rvz", bp * 2 + 1)])
                            elif sec == 3:
                                act(rg[:, bp * 2:bp * 2 + 2, cols], src, AF.Silu, [PS(bk)], [("rg", bp * 2), ("rg", bp * 2 + 1)])
                            else:
                                for bb in range(2):
                                    b = bp * 2 + bb
                                    cp(vc[:, t * 4 + b, half * 2:half * 2 + 2, 0:128],
                                       ps[:, bk, bb * 256:(bb + 1) * 256].rearrange("p (a b) -> p a b", a=2), [PS(bk)], [("vc", t * 4 + b, half)])
            rope_flush()

            units = []
            ubank = [0]

            def make_units():
                for sec in (4, 5, 6):
                    for half in range(2):
                        pj = sec * 2 + half
                        holder = {}

                        def get_w(pj=pj, holder=holder):
                            if "slot" not in holder:
                                holder["slot"] = load_piece("in", pj)
                            slot = holder["slot"]
                            return slot, ring[:, slot, :].rearrange("p (k n) -> p k n", k=8)

                        for sub in range(2):
                            def unit(sec=sec, half=half, sub=sub, get_w=get_w):
                                slot, w = get_w()
                                bk = 5 + ubank[0] % 2
                                ubank[0] += 1
                                if sec in (4, 5):
                                    ch = half * 2 + sub
                                    for k in range(8):
                                        mm(ps[:, bk, :], w[:, k, sub * 128:(sub + 1) * 128], hT[:, k, :], k == 0, k == 7,
                                           [*RK(slot), ("hT", k)], [PS(bk)])
                                    if sec == 4:
                                        act(dqT[:, ch, :], ps[:, bk, :], AF.Copy, [PS(bk)], [actk(20 + ch)], scale=0.125)
                                    else:
                                        act(kc[:, ch, t * TT:(t + 1) * TT], ps[:, bk, :], AF.Copy, [PS(bk)], [("kc", ch, t)])
                                else:
                                    bp = sub
                                    for bb in range(2):
                                        b = bp * 2 + bb
                                        for k in range(8):
                                            mm(ps[:, bk, bb * 256:(bb + 1) * 256], hT[:, k, b * 128:(b + 1) * 128], w[:, k, :], k == 0, k == 7,
                                               [*RK(slot), ("hT", k)], [PS(bk)])
                                    for bb in range(2):
                                        b = bp * 2 + bb
                                        act(vc[:, t * 4 + b, half * 2:half * 2 + 2, 0:128],
                                            ps[:, bk, bb * 256:(bb + 1) * 256].rearrange("p (a b) -> p a b", a=2), AF.Copy,
                                            [PS(bk)], [("vc", t * 4 + b, half)])
                            units.append(unit)

            make_units()

            def run_units(n):
                for _ in range(n):
                    if units:
                        units.pop(0)()

            ps7b = ps[:, 7, 0:256].bitcast(BF16).rearrange("p (a b) -> p a b", a=4)
            for b in range(4):
                for c in range(4):
                    tr(ps7b[:, c, :], rkT(c)[:, b * 128:(b + 1) * 128], ident_b[:, :], [actk(8 + c), "ident_b"], [PS(7)])
                act(rk_tok[:, b, :].rearrange("p (a b) -> p a b", a=4), ps7b, AF.Copy, [PS(7)], [actk(12 + b)])

            def ret_S(b):
                ai = b % 2
                for hh in range(8):
                    c = hh // 2
                    pb = (hh % 2) * 64
                    mm(ps[:, hh % 2, (hh // 2) * 128:(hh // 2 + 1) * 128], rkT(c)[pb:pb + 64, b * 128:(b + 1) * 128],
                       rqT(c)[pb:pb + 64, b * 128:(b + 1) * 128], True, True, [actk(8 + c), actk(0 + c)], [PS(hh % 2)])
                o = COFF["DT"][0]
                for g in range(2):
                    tt(AT[:, ai, :, :].rearrange("p (a two) b -> p a two b", two=2)[:, :, g, :], ps[:, g, :].rearrange("p (a b) -> p a b", a=4),
                       cst[:, o:o + 1024].rearrange("p (a two b) -> p a two b", two=2, b=128)[:, :, g, :], ALU.mult,
                       [PS(g), "cst"], [("AT", ai, g)])

            def ret_acc(b):
                ai = b % 2
                bk = 2 + b % 2
                for hh in range(8):
                    c = hh // 2
                    pb = (hh % 2) * 64
                    mm(ps[:, bk, hh * 64:(hh + 1) * 64], AT[:, ai, hh, :], rv_tok[:, b, hh * 64:(hh + 1) * 64], True, False,
                       [("AT", ai, hh % 2), actk(16 + b)], [PS(bk)])
                    mm(ps[:, bk, hh * 64:(hh + 1) * 64], rqxT(c)[pb:pb + 64, b * 128:(b + 1) * 128],
                       Rb[pb:pb + 64, c, (hh % 2) * 64:(hh % 2 + 1) * 64], False, True, [actk(4 + c), "Rb"], [PS(bk)])
                for c in range(4):
                    mm(ps[:, 4, c * 128:(c + 1) * 128], rk_tok[:, b, c * 128:(c + 1) * 128], rvz[:, b, c * 128:(c + 1) * 128], True, True,
                       [actk(12 + b), ("rvz", b)], [PS(4)])

            def ret_R(b):
                og = COFF["gc"][0]
                gcb = cst[:, og:og + 4].unsqueeze(2).broadcast_to([128, 4, 128])
                tt(R[:, :, :], R[:, :, :], gcb, ALU.mult, ["R", "cst"], ["R"], eng=pool_eng())
                tt(R[:, :, :], R[:, :, :], ps[:, 4, :].rearrange("p (a b) -> p a b", a=4), ALU.add, ["R", PS(4)], ["R"])
                act(Rb[:, :, :], R[:, :, :], AF.Copy, ["R"], ["Rb"])

            def ret_LN(b):
                ai = b % 2
                bk = 2 + b % 2
                acc3 = ps[:, bk, :].rearrange("p (a b) -> p a b", a=8)
                act(tmpA[:, :], ps[:, bk, :], AF.Square, [PS(bk)], ["tmpA"])
                red(sm[:, 0:8], acc3, [PS(bk)], ["sm0"])
                red(sm[:, 8:16], tmpA[:, :].rearrange("p (a b) -> p a b", a=8), ["tmpA"], ["sm1"])
                ts(sm[:, 16:24], sm[:, 0:8], 1.0 / 64, ALU.mult, ["sm0"], ["sm2"])
                tt(sm[:, 24:32], sm[:, 16:24], sm[:, 16:24], ALU.mult, ["sm2"], ["sm3"])
                stt(sm[:, 32:40], sm[:, 8:16], 1.0 / 64, sm[:, 24:32], ALU.mult, ALU.subtract, ["sm1", "sm3"], ["sm4"])
                act(sm[:, 40:48], sm[:, 32:40], AF.Ln, ["sm4", "cst"], ["sm5"], bias=C("eps"))
                act(sm[:, 48:56], sm[:, 40:48], AF.Exp, ["sm5"], ["sm6"], scale=-0.5)
                tt(tmpB[:, :].rearrange("p (a b) -> p a b", a=8), acc3, sm[:, 16:24].unsqueeze(2).broadcast_to([128, 8, 64]),
                   ALU.subtract, [PS(bk), "sm2"], ["tmpB"])
                tt(tmpB[:, :].rearrange("p (a b) -> p a b", a=8), tmpB[:, :].rearrange("p (a b) -> p a b", a=8),
                   sm[:, 48:56].unsqueeze(2).broadcast_to([128, 8, 64]), ALU.mult, ["tmpB", "sm6"], ["tmpB"], eng=pool_eng())
                tt(actT[:, 12 + b, :], tmpB[:, :], rg[:, b, :], ALU.mult, ["tmpB", ("rg", b)], [actk(12 + b)], eng=pool_eng())

            def ret_T(b):
                ai = b % 2
                for c in range(4):
                    tr(ps7b[:, c, :], actT[:, 12 + b, c * 128:(c + 1) * 128], ident_b[:, :], [actk(12 + b), "ident_b"], [PS(7)])
                ogs = COFF["gsT"][0]
                tt(yT[:, 0:4, b * 128:(b + 1) * 128], ps7b, cst[:, ogs:ogs + 4].unsqueeze(2).broadcast_to([128, 4, 128]), ALU.mult,
                   [PS(7), "cst"], [("hT", c_) for c_ in range(4)])

            ret_S(0)
            for b in range(4):
                ret_acc(b)
                ret_R(b)
                run_units(1)
                if b + 1 < 4:
                    ret_S(b + 1)
                ret_LN(b)
                run_units(2)
            run_units(len(units))

            o31 = COFF["b31"][0]
            nj = 4 * t + 4
            def s_stage(h, j, gi):
                qlo = max(j, 4 * t) - 4 * t
                c0 = qlo * 128
                pr = (gi % 2) * 2
                for m in range(2):
                    mm(ps[:, pr + m, c0:512], kc[m * 64:(m + 1) * 64, h, j * 128:(j + 1) * 128], dqT[m * 64:(m + 1) * 64, h, c0:512],
                       True, True, [("kc", h, j // 4), actk(20 + h)], [PS(pr + m)])
                if j >= 4 * t - 1:
                    if j == 4 * t - 1:
                        a_, b_, w_ = 0, 128, slice(128, 256)
                    else:
                        r = j - 4 * t
                        wd = 256 if r < 3 else 128
                        a_, b_, w_ = r * 128, r * 128 + wd, slice(0, wd)
                    n_ = b_ - a_
                    tt(ps[:, pr:pr + 2, a_:b_], ps[:, pr:pr + 2, a_:b_], badj[:, h, w_].unsqueeze(1).broadcast_to([128, 2, n_]), ALU.add,
                       [PS(pr), PS(pr + 1), ("badj", h)], [PS(pr), PS(pr + 1)])
                ei = gi % 4
                for m in range(2):
                    act(E[:, ei, m, c0:512], ps[:, pr + m, c0:512], AF.Exp, [PS(pr + m), "cst"], [("E", ei, m)],
                        bias=cst[:, o31 + h:o31 + h + 1])

            def pv_stage(h, j, gi):
                qlo = max(j, 4 * t) - 4 * t
                c0 = qlo * 128
                ei = gi % 4
                mm(ps[:, 4, c0:512], vc[:, j, h, :], E[:, ei, 0, c0:512], j == 0, j == nj - 1,
                   [("E", ei, 0), ("vc", j, h // 2)], [PS(4)])
                mm(ps[:, 6, c0:512], ones_b[:, :], E[:, ei, 0, c0:512], j == 0, j == nj - 1, [("E", ei, 0), "ones_b"], [PS(6)])
                mm(ps[:, 5, c0:512], vc[:, j, h, :], E[:, ei, 1, c0:512], j == 0, j == nj - 1,
                   [("E", ei, 1), ("vc", j, h // 2)], [PS(5)])
                if j == 0:
                    cp(Esum[:, :], E[:, ei, 1, :], [("E", ei, 1)], ["Esum"])
                else:
                    tt(Esum[:, c0:512], Esum[:, c0:512], E[:, ei, 1, c0:512], ALU.add, ["Esum", ("E", ei, 1)], ["Esum"])
                if j == nj - 1:
                    mm(ps[:, 7, :], ones_f[:, :], Esum[:, :], True, True, ["Esum", "ones_f"], [PS(7)])

            def epilogue(h):
                act(Osb[:, 0, :], ps[:, 4, :], AF.Copy, [PS(4)], [("Osb", 0)])
                cp(Osb[:, 1, :], ps[:, 5, :], [PS(5)], [("Osb", 1)])
                act(tmpA[:, :], ps[:, 6, :], AF.Copy, [PS(6)], ["tmpA"])
                cp(tmpB[:, :], ps[:, 7, :], [PS(7)], ["tmpB"])
                tt(rstd[:, :], tmpA[:, :], tmpB[:, :], ALU.mult, ["tmpA", "tmpB"], ["rstd"])
                tt(Osb[:, 0, :], Osb[:, 0, :], tmpB[:, :], ALU.mult, [("Osb", 0), "tmpB"], [("Osb", 0)])
                tt(Osb[:, 1, :], Osb[:, 1, :], tmpA[:, :], ALU.mult, [("Osb", 1), "tmpA"], [("Osb", 1)])
                stt(Osb[:, 0, :], Osb[:, 1, :], neglam, Osb[:, 0, :], ALU.mult, ALU.add, [("Osb", 0), ("Osb", 1), "neglam"], [("Osb", 0)])
                stt(rstd[:, :], rstd[:, :], EPS, rstd[:, :], ALU.mult, ALU.mult, ["rstd"], ["rstd"])

                def p2():
                    act(sq[:, 0, :], Osb[:, 0, :], AF.Square, [("Osb", 0)], [("sq", 0)])
                    mm(ps[:, 0, :], ones_b[:, :], sq[:, 0, :], True, True, [("sq", 0), "ones_b"], [PS(0)])
                    stt(tmpA[:, :], ps[:, 0, :], 1.0 / 128, rstd[:, :], ALU.mult, ALU.add, [PS(0), "rstd", "tmpA"], ["tmpA"])

                def p3():
                    act(tmpA[:, :], tmpA[:, :], AF.Ln, ["tmpA"], ["tmpA"])
                    act(tmpA[:, :], tmpA[:, :], AF.Exp, ["tmpA"], ["tmpA"], scale=-0.5)

                def p4():
                    stt(yT[:, 4 + h, :], Osb[:, 0, :], gsub[:, h:h + 1], tmpA[:, :], ALU.mult, ALU.mult, [("Osb", 0), "tmpA", "gsub"], [("hT", 4 + h)])

                d0 = 2 if nj < 8 else 5
                deferred.append([d0, p2])
                deferred.append([d0 + 1, p3])
                deferred.append([d0 + 2, p4])

            deferred = []

            def tick():
                for d in list(deferred):
                    d[0] -= 1
                    if d[0] <= 0:
                        d[1]()
                        deferred.remove(d)

            if t == 0:
                mslots = {}
                for idx, j_ in enumerate(range(20, 32)):
                    def ld(j_=j_):
                        mslots[j_] = load_piece("ada", j_)

                    def cmp_(j_=j_):
                        mod_piece_mm(j_, mslots[j_], 7)
                    deferred.append([idx + 1, ld])
                    deferred.append([idx + 3, cmp_])
                deferred.append([15, mod_B2bc_finish])
            gst = [(h, j) for h in range(4) for j in range(nj)]
            s_stage(gst[0][0], gst[0][1], 0)
            for gi, (h, j) in enumerate(gst):
                if gi + 1 < len(gst):
                    s_stage(gst[gi + 1][0], gst[gi + 1][1], gi + 1)
                pv_stage(h, j, gi)
                if h == 3 and j == 0:
                    for b_ in range(4):
                        deferred.append([b_ + 1 if nj >= 8 else 1, (lambda b_=b_: ret_T(b_))])
                if j == nj - 1:
                    epilogue(h)
                tick()
            while deferred:
                tick()

            wslots = [load_piece("out", pj) for pj in range(4)]
            for r in range(2):
                for k in range(8):
                    for i in range(4):
                        dc = r * 4 + i
                        slot = wslots[dc // 2]
                        w = ring[:, slot, :].rearrange("p (k n) -> p k n", k=8)
                        mm(ps[:, i, :], w[:, k, (dc % 2) * 128:(dc % 2 + 1) * 128], yT[:, k, :], k == 0, k == 7, [*RK(slot), ("hT", k)], [PS(i)])
                for i in range(4):
                    dc = r * 4 + i
                    if dc > 0:
                        norm_mm(dc - 1)
                    stt(xT[:, dc, :], ps[:, i, :], der[:, 24 + dc:25 + dc], xT[:, dc, :], ALU.mult, ALU.add,
                        [PS(i), XK(dc), "der3"], [XK(dc)])
                    norm_sq(dc)
            norm_mm(7)

        all_stores = []
        issue_x_load(0)
        xpar[0] = 0
        norm1_stats(7)
        norm_rstd(7, rstd1, "rstd1")
        norm_apply(der[:, 0:8], MOD(0), ["der0", "modA"], rstd1, "rstd1")
        pending_out = []
        for t in range(NT):
            cur_tile[0] = t
            xpar[0] = t % 2
            ffn_in("1i", extra_units=pending_out)
            pending_out = []
            if t == 0:
                mod_B1()
            ffn_out("1o", der[:, 8:16], ["der1"])
            norm_rstd()
            if t == 0:
                mod_B2a()
            norm_apply(der[:, 16:24], MOD(3), ["der2", "modB"])
            if t + 1 < NT:
                issue_x_load(t + 1)
            mixer(t)
            norm_rstd()
            norm_apply(der[:, 32:40], MOD(6), ["der4"] + [("mod", j_) for j_ in range(24, 28)])
            ffn_in("2i")
            if t == 0:
                mod_B2d()
            if t + 1 < NT:
                xpar[0] = (t + 1) % 2
                norm1_stats(7)
                norm_rstd(7, rstd1, "rstd1")
                norm_apply(der[:, 0:8], MOD(0), ["der0", "modA"], rstd1, "rstd1")
                xpar[0] = t % 2
            ffn_out("2o", der[:, 40:48], ["der5"])
            def fin(t=t):
                norm_rstd()
                store_tile(t)
            pending_out = [fin]
        for u in pending_out:
            u()

        S.emit(nc, final_waits=S.dma_ops["pool"][-16:])
    return nc


_PROG_CACHE = {}


def kernel(**inputs):
    S_len = inputs["x"].shape[1]
    B = inputs["x"].shape[0]
    if S_len not in _PROG_CACHE:
        _PROG_CACHE[S_len] = build_program(S_len)
    nc = _PROG_CACHE[S_len]
    inp = {k: np.asarray(v) for k, v in inputs.items()}
    in_maps = [make_core_inputs(inp, b, S_len) for b in range(B)]
    res = run_bass_kernel_spmd(nc, in_maps, core_ids=list(range(B)))
    out = np.stack([np.ascontiguousarray(np.asarray(r["out"], dtype=np.float32).T) for r in res.results], axis=0)
    return out
```
